# Optimizing a Trainium2 kernel written in Bass

```python
import jax, jax.numpy as jnp
from jax import lax
import numpy as np

D_MODEL = 1024
BATCH = 8
SEQ = 4096
DEPTH = 4

N_HEADS = 8
HEAD_DIM = 128
N_KV_HEADS = 2
ROPE_FRACTION_DIV = 4
ROPE_THETA = 500000.0
IDX_HEADS = 8
IDX_DIM = 64
TOPK_MAX = 256
Q_BLOCK = 128
CONF_WIDTH = D_MODEL
CONF_KERNEL = 31
SC_WIDTH = D_MODEL
SC_KERNEL = 3
FFN_DIM = 2816
FFN_KERNEL = 3
N_BRANCH = 3
NORM_EPS = 1e-6

Q_COLS = N_HEADS * HEAD_DIM
KV_COLS = N_KV_HEADS * HEAD_DIM
IQ_COLS = IDX_HEADS * IDX_DIM
IK_COLS = IDX_DIM
IW_COLS = IDX_HEADS
A_COLS = 2 * CONF_WIDTH
C_COLS = 3 * SC_WIDTH
G_COLS = N_BRANCH * D_MODEL
_IN_SIZES = (Q_COLS, KV_COLS, KV_COLS, IQ_COLS, IK_COLS, IW_COLS, A_COLS, C_COLS, G_COLS)
N_IN = sum(_IN_SIZES)
IN_SPLITS = tuple(sum(_IN_SIZES[:i + 1]) for i in range(len(_IN_SIZES) - 1))

kernel_name = "hybrid_gated_conformer_dsa_shortconv"


def rms_norm(x, g):
    xf = x.astype(jnp.float32)
    r = lax.rsqrt(jnp.mean(xf * xf, axis=-1, keepdims=True) + NORM_EPS)
    return (xf * r * g.astype(jnp.float32)).astype(x.dtype)


def layer_norm(x, g, b):
    xf = x.astype(jnp.float32)
    mu = jnp.mean(xf, axis=-1, keepdims=True)
    xc = xf - mu
    r = lax.rsqrt(jnp.mean(xc * xc, axis=-1, keepdims=True) + NORM_EPS)
    return (xc * r * g.astype(jnp.float32) + b.astype(jnp.float32)).astype(x.dtype)


def causal_dwconv(x, w):
    K, C = w.shape
    return lax.conv_general_dilated(
        x, w[:, None, :].astype(x.dtype), window_strides=(1,), padding=[(K - 1, 0)],
        dimension_numbers=("NWC", "WIO", "NWC"), feature_group_count=C)


def rope_tables(seq_len, rot_dim):
    pos = jnp.arange(seq_len, dtype=jnp.float32)
    inv_freq = jnp.power(ROPE_THETA, -jnp.arange(0, rot_dim, 2, dtype=jnp.float32) / rot_dim)
    ang = pos[:, None] * inv_freq[None, :]
    return jnp.cos(ang), jnp.sin(ang)


def partial_rope(x, cos, sin):
    half = cos.shape[-1]
    xf = x.astype(jnp.float32)
    x1, x2, xp = xf[..., :half], xf[..., half:2 * half], xf[..., 2 * half:]
    c = cos[None, :, None, :]
    s = sin[None, :, None, :]
    out = jnp.concatenate([x1 * c - x2 * s, x2 * c + x1 * s, xp], axis=-1)
    return out.astype(x.dtype)


def dsa_attention(q, k, v, iq, ik, iw):
    B, S, H, dh = q.shape
    G = k.shape[2]
    rep = H // G
    k_sel = min(TOPK_MAX, S // 4)
    n_blocks = S // Q_BLOCK
    key_pos = jnp.arange(S)
    scale = HEAD_DIM ** -0.5
    gather = jax.vmap(lambda arr, idx: arr[idx])

    def block(i):
        t0 = i * Q_BLOCK
        qb = lax.dynamic_slice_in_dim(q, t0, Q_BLOCK, axis=1)
        iqb = lax.dynamic_slice_in_dim(iq, t0, Q_BLOCK, axis=1)
        iwb = lax.dynamic_slice_in_dim(iw, t0, Q_BLOCK, axis=1)
        q_pos = t0 + jnp.arange(Q_BLOCK)
        rel = jax.nn.relu(jnp.einsum('bthd,bsd->bths', iqb, ik).astype(jnp.float32))
        score = jnp.einsum('bths,bth->bts', rel, iwb.astype(jnp.float32))
        causal = key_pos[None, :] <= q_pos[:, None]
        score = jnp.where(causal[None], score, -jnp.inf)
        _, idx = lax.top_k(score, k_sel)
        valid = idx <= q_pos[None, :, None]
        flat = idx.reshape(B, Q_BLOCK * k_sel)
        kg = gather(k, flat).reshape(B, Q_BLOCK, k_sel, G, dh)
        vg = gather(v, flat).reshape(B, Q_BLOCK, k_sel, G, dh)
        qg = qb.reshape(B, Q_BLOCK, G, rep, dh)
        logits = jnp.einsum('btgrd,btkgd->btgrk', qg, kg).astype(jnp.float32) * scale
        logits = jnp.where(valid[:, :, None, None, :], logits, -jnp.inf)
        p = jax.nn.softmax(logits, axis=-1).astype(v.dtype)
        o = jnp.einsum('btgrk,btkgd->btgrd', p, vg)
        return o.reshape(B, Q_BLOCK, H * dh)

    out = lax.map(block, jnp.arange(n_blocks))
    return out.transpose(1, 0, 2, 3).reshape(B, S, H * dh)


def hybrid_layer(x, rope_attn, rope_idx, norm1_g, w_in, q_norm_g, k_norm_g, w_attn_out,
                 conf_conv_w, conf_conv_b, conf_ln_g, conf_ln_b, w_conf_out,
                 sc_conv_w, w_sc_out, w_o, norm2_g, w_up, ffn_conv_w, ffn_conv_b, w_down):
    B, S, _ = x.shape
    h = rms_norm(x, norm1_g)
    proj = h @ w_in
    q, k, v, iq, ik, iw, a_in, c_in, g_in = jnp.split(proj, IN_SPLITS, axis=-1)

    q = partial_rope(rms_norm(q.reshape(B, S, N_HEADS, HEAD_DIM), q_norm_g), *rope_attn)
    k = partial_rope(rms_norm(k.reshape(B, S, N_KV_HEADS, HEAD_DIM), k_norm_g), *rope_attn)
    v = v.reshape(B, S, N_KV_HEADS, HEAD_DIM)
    iq = partial_rope(iq.reshape(B, S, IDX_HEADS, IDX_DIM), *rope_idx)
    ik = partial_rope(ik[:, :, None, :], *rope_idx)[:, :, 0, :]
    iw = iw * (IDX_HEADS ** -0.5 * IDX_DIM ** -0.5)
    y_attn = dsa_attention(q, k, v, iq, ik, iw) @ w_attn_out

    a_val, a_gate = jnp.split(a_in, 2, axis=-1)
    a = causal_dwconv(a_val * jax.nn.sigmoid(a_gate), conf_conv_w) + conf_conv_b
    y_conf = jax.nn.silu(layer_norm(a, conf_ln_g, conf_ln_b)) @ w_conf_out

    c_b, c_c, c_x = jnp.split(c_in, 3, axis=-1)
    y_sc = (c_b * causal_dwconv(c_c * c_x, sc_conv_w)) @ w_sc_out

    gates = jax.nn.sigmoid(g_in.reshape(B, S, N_BRANCH, D_MODEL).astype(jnp.float32)).astype(x.dtype)
    merged = gates[:, :, 0] * y_conf + gates[:, :, 1] * y_attn + gates[:, :, 2] * y_sc
    x = x + merged @ w_o

    u = causal_dwconv(rms_norm(x, norm2_g) @ w_up, ffn_conv_w) + ffn_conv_b
    u_gate, u_val = jnp.split(u, 2, axis=-1)
    return x + (jax.nn.silu(u_gate) * u_val) @ w_down


def setup_inputs(seed: int = 0) -> dict:
    key = jax.random.key(seed)
    ks = jax.random.split(key, 20)
    f32 = jnp.float32

    def nrm(k, shape, scale):
        return jax.random.normal(k, shape, f32) * scale

    L = DEPTH
    return {
        "x": nrm(ks[0], (BATCH, SEQ, D_MODEL), 1.0),
        "norm1_g": 1.0 + nrm(ks[1], (L, D_MODEL), 0.05),
        "w_in": nrm(ks[2], (L, D_MODEL, N_IN), D_MODEL ** -0.5),
        "q_norm_g": 1.0 + nrm(ks[3], (L, HEAD_DIM), 0.05),
        "k_norm_g": 1.0 + nrm(ks[4], (L, HEAD_DIM), 0.05),
        "w_attn_out": nrm(ks[5], (L, Q_COLS, D_MODEL), Q_COLS ** -0.5),
        "conf_conv_w": nrm(ks[6], (L, CONF_KERNEL, CONF_WIDTH), CONF_KERNEL ** -0.5),
        "conf_conv_b": nrm(ks[7], (L, CONF_WIDTH), 0.01),
        "conf_ln_g": 1.0 + nrm(ks[8], (L, CONF_WIDTH), 0.05),
        "conf_ln_b": nrm(ks[9], (L, CONF_WIDTH), 0.01),
        "w_conf_out": nrm(ks[10], (L, CONF_WIDTH, D_MODEL), CONF_WIDTH ** -0.5),
        "sc_conv_w": nrm(ks[11], (L, SC_KERNEL, SC_WIDTH), SC_KERNEL ** -0.5),
        "w_sc_out": nrm(ks[12], (L, SC_WIDTH, D_MODEL), SC_WIDTH ** -0.5),
        "w_o": nrm(ks[13], (L, D_MODEL, D_MODEL), D_MODEL ** -0.5),
        "norm2_g": 1.0 + nrm(ks[14], (L, D_MODEL), 0.05),
        "w_up": nrm(ks[15], (L, D_MODEL, 2 * FFN_DIM), D_MODEL ** -0.5),
        "ffn_conv_w": nrm(ks[16], (L, FFN_KERNEL, 2 * FFN_DIM), FFN_KERNEL ** -0.5),
        "ffn_conv_b": nrm(ks[17], (L, 2 * FFN_DIM), 0.01),
        "w_down": nrm(ks[18], (L, FFN_DIM, D_MODEL), FFN_DIM ** -0.5),
    }


def reference(x, norm1_g, w_in, q_norm_g, k_norm_g, w_attn_out, conf_conv_w, conf_conv_b,
              conf_ln_g, conf_ln_b, w_conf_out, sc_conv_w, w_sc_out, w_o, norm2_g, w_up,
              ffn_conv_w, ffn_conv_b, w_down):
    S = x.shape[1]
    rope_attn = rope_tables(S, HEAD_DIM // ROPE_FRACTION_DIV)
    rope_idx = rope_tables(S, IDX_DIM // ROPE_FRACTION_DIV)
    for l in range(DEPTH):
        x = hybrid_layer(x, rope_attn, rope_idx, norm1_g[l], w_in[l], q_norm_g[l], k_norm_g[l],
                         w_attn_out[l], conf_conv_w[l], conf_conv_b[l], conf_ln_g[l], conf_ln_b[l],
                         w_conf_out[l], sc_conv_w[l], w_sc_out[l], w_o[l], norm2_g[l], w_up[l],
                         ffn_conv_w[l], ffn_conv_b[l], w_down[l])
    return x
```

```python
import numpy as np
import concourse.bass as bass
import concourse.mybir as mybir
from concourse.bass_utils import run_bass_kernel_spmd

F32 = mybir.dt.float32
BF16 = mybir.dt.bfloat16
ALU = mybir.AluOpType
AF = mybir.ActivationFunctionType
AX = mybir.AxisListType

D = 1024
SEQ = 4096
DEPTH = 4
NCORES = 8
T = 512
TT = 4
NUNITS = 46
NSLOT = 3
EPS = 1e-6
NEG = -1.0e4
NIT = 10
TOPK = 256
FFN = 2816
NV = 8 + 8 + 248 + 8 + 8 + 8 + 24 + 132 + 44
V_N1, V_N2, V_CW, V_CB, V_LG, V_LB, V_SW, V_FW, V_FB = 0, 8, 16, 264, 272, 280, 288, 312, 444
ATTN_SCALE = 128 ** -0.5
IW_SCALE = (8 ** -0.5) * (64 ** -0.5)


class Atom:
    __slots__ = ("w", "r")

    def __init__(self):
        self.w = None
        self.r = {}


class Buf:
    def __init__(self, atoms=None):
        self.atoms = atoms if atoms is not None else [Atom()]


class V:
    __slots__ = ("ap", "bufs")

    def __init__(self, ap, bufs):
        self.ap = ap
        self.bufs = bufs

    def __getitem__(self, k):
        return V(self.ap[k], self.bufs)

    def bitcast(self, dt):
        return V(self.ap.bitcast(dt), self.bufs)

    def rr(self, pat, **kw):
        return V(self.ap.rearrange(pat, **kw), self.bufs)

    def bc(self, axis, shape):
        return V(self.ap.unsqueeze(axis).broadcast_to(shape), self.bufs)


NDMA = 24


class Sched:
    ENGS = ("pe", "act", "dve", "pool", "sp")

    def __init__(self):
        self.ops = {e: [] for e in self.ENGS}
        self.cnt = {e: 0 for e in self.ENGS}
        self.waited = {e: {} for e in self.ENGS}
        self.dma_val = [0] * NDMA
        self.dma_next = {"sp": 0, "pool": 0, "act": 0}

    def op(self, eng, fn, reads=(), writes=(), dma=False):
        deps = {}

        def add(tok):
            if tok is None:
                return
            sk, val, teng = tok
            if teng == "pe" and eng == "pe" and not dma:
                return
            if deps.get(sk, 0) < val:
                deps[sk] = val

        for v in reads:
            for b in v.bufs:
                for a in b.atoms:
                    add(a.w)
        for v in writes:
            for b in v.bufs:
                for a in b.atoms:
                    add(a.w)
                    for sk, (val, teng) in a.r.items():
                        add((sk, val, teng))
        waits = []
        wd = self.waited[eng]
        for sk, val in deps.items():
            if wd.get(sk, 0) >= val:
                continue
            wd[sk] = val
            waits.append((sk, val))
        if dma:
            kk = self.dma_next[eng]
            self.dma_next[eng] = kk + 1
            if eng == "sp":
                k = kk % 12
            elif eng == "act":
                k = 12 + kk % 6
            else:
                k = 18 + kk % (NDMA - 18)
            prev = self.dma_val[k]
            sk = ("d", k)
            if prev > 0 and wd.get(sk, 0) < prev:
                wd[sk] = prev
                waits.append((sk, prev))
            self.dma_val[k] += 16
            tok = (sk, self.dma_val[k], "dma")
        else:
            self.cnt[eng] += 1
            tok = (("e", eng), self.cnt[eng], eng)
        self.ops[eng].append((waits, fn, tok, dma))
        for v in reads:
            for b in v.bufs:
                for a in b.atoms:
                    cur = a.r.get(tok[0])
                    if cur is None or cur[0] < tok[1]:
                        a.r[tok[0]] = (tok[1], tok[2])
        for v in writes:
            for b in v.bufs:
                for a in b.atoms:
                    a.w = tok
                    a.r = {}
        return tok

    def final_wait(self, eng, toks):
        waits = []
        for sk, val, _ in toks:
            waits.append((sk, val))
        self.ops[eng].append((waits, None, None, False))

    def emit(self, nc, block, sems):
        def semof(sk):
            return sems[sk]

        for eng, attr in (("pe", "tensor"), ("act", "scalar"), ("dve", "vector"), ("pool", "gpsimd"), ("sp", "sync")):
            ops = self.ops[eng]

            def body(e, ops=ops):
                for waits, fn, tok, dma in ops:
                    for sk, val in waits:
                        e.wait_ge(semof(sk), val)
                    if fn is None:
                        continue
                    ins = fn(e)
                    ins.then_inc(semof(tok[0]), 16 if dma else 1)

            getattr(block, attr)(body)


class Builder:
    def __init__(self, nc, n_layers, n_groups):
        self.nc = nc
        self.S = Sched()
        self.n_layers = n_layers
        self.n_groups = n_groups

    def mm(self, out, lhsT, rhs, start=True, stop=True):
        o, l, r = out.ap, lhsT.ap, rhs.ap
        self.S.op("pe", lambda e: e.matmul(o, l, r, start=start, stop=stop), reads=(lhsT, rhs), writes=(out,))

    def tr(self, out, in_, ident):
        o, i, d = out.ap, in_.ap, ident.ap
        self.S.op("pe", lambda e: e.transpose(o, i, d), reads=(in_, ident), writes=(out,))

    def act(self, out, in_, func, bias=None, scale=None, accum=None):
        kw = {}
        reads = [in_]
        if bias is not None:
            if isinstance(bias, V):
                kw["bias"] = bias.ap
                reads.append(bias)
            else:
                kw["bias"] = bias
        if scale is not None:
            if isinstance(scale, V):
                kw["scale"] = scale.ap
                reads.append(scale)
            else:
                kw["scale"] = scale
        writes = [out]
        if accum is not None:
            kw["accum_out"] = accum.ap
            writes.append(accum)
        o, i = out.ap, in_.ap
        self.S.op("act", lambda e: e.activation(o, i, func, **kw), reads=reads, writes=writes)

    def ts(self, eng, out, in0, s1, s2, op0, op1=None, accum=None):
        reads = [in0]
        a1 = s1
        if isinstance(s1, V):
            a1 = s1.ap
            reads.append(s1)
        a2 = s2
        if isinstance(s2, V):
            a2 = s2.ap
            reads.append(s2)
        writes = [out]
        kw = {}
        if accum is not None:
            kw["accum_out"] = accum.ap
            writes.append(accum)
        if op1 is not None:
            kw["op1"] = op1
        o, i = out.ap, in0.ap
        self.S.op(eng, lambda e: e.tensor_scalar(o, i, a1, a2, op0, **kw), reads=reads, writes=writes)

    def tt(self, eng, out, in0, in1, op):
        o, a, b = out.ap, in0.ap, in1.ap
        self.S.op(eng, lambda e: e.tensor_tensor(o, a, b, op), reads=(in0, in1), writes=(out,))

    def stt(self, out, in0, scalar, in1, op0, op1):
        reads = [in0, in1]
        sc = scalar
        if isinstance(scalar, V):
            sc = scalar.ap
            reads.append(scalar)
        o, a, b = out.ap, in0.ap, in1.ap
        self.S.op("dve", lambda e: e.scalar_tensor_tensor(o, a, sc, b, op0, op1), reads=reads, writes=(out,))

    def copy(self, eng, out, in_):
        o, i = out.ap, in_.ap
        if eng == "act":
            self.S.op("act", lambda e: e.copy(o, i), reads=(in_,), writes=(out,))
        else:
            self.S.op(eng, lambda e: e.tensor_copy(o, i), reads=(in_,), writes=(out,))

    def recip(self, out, in_):
        o, i = out.ap, in_.ap
        self.S.op("dve", lambda e: e.reciprocal(o, i), reads=(in_,), writes=(out,))

    def memset(self, eng, out, val):
        o = out.ap
        self.S.op(eng, lambda e: e.memset(o, val), writes=(out,))

    def reduce(self, out, in_, op, absval=False):
        o, i = out.ap, in_.ap
        kw = {"apply_absolute_value": True} if absval else {}
        self.S.op("dve", lambda e: e.tensor_reduce(o, i, AX.X, op, **kw), reads=(in_,), writes=(out,))

    def dma(self, out, in_, eng="sp"):
        o, i = out.ap, in_.ap
        return self.S.op(eng, lambda e: e.dma_start(out=o, in_=i), reads=(in_,), writes=(out,), dma=True)


def build_program(n_layers=DEPTH, n_groups=SEQ // T):
    from contextlib import ExitStack

    nc = bass.Bass("TRN2", target_bir_lowering=False)
    B = Builder(nc, n_layers, n_groups)
    S = B.S
    L = n_layers

    def dram(name, shape, dt, kind):
        return nc.dram_tensor(name, shape, dt, kind=kind).ap()

    xT_d = dram("xT", [8, 128, SEQ], F32, "ExternalInput")
    wst_d = dram("wst", [L, NUNITS, 128, 4096], F32, "ExternalInput")
    vecs_d = dram("vecs", [128, L, NV], F32, "ExternalInput")
    gqk_d = dram("gqk", [128, L, 256], F32, "ExternalInput")
    rope_d = dram("rope", [128, 32, 72], F32, "ExternalInput")
    cst_d = dram("cst", [128, 384 + NIT + 2], F32, "ExternalInput")
    yT_d = dram("yT", [8, 128, SEQ], F32, "ExternalOutput")
    wb_d = dram("wb", [L, NUNITS, 128, 4096], BF16, "Internal")
    res_d = [dram("res%d" % i, [8, 128, SEQ], F32, "Internal") for i in range(2)]

    xT_v = [V(xT_d, [Buf()]) for _ in range(SEQ // T)]
    yT_v = [V(yT_d, [Buf()]) for _ in range(SEQ // T)]
    res_v = [[V(r, [Buf()]) for _ in range(SEQ // T)] for r in res_d]
    wst_v = V(wst_d, [Buf()])
    wb_units = [[V(wb_d[l, u], [Buf()]) for u in range(NUNITS)] for l in range(L)]

    es = ExitStack()
    with es:
        def sb(name, shape, dt):
            t = es.enter_context(nc.sbuf_tensor(name, shape, dt))
            return V(t[:], [Buf()])

        class Chunked:
            def __init__(self, name, dt):
                t = es.enter_context(nc.sbuf_tensor(name, [128, 8, T], dt))
                self.ap = t[:]
                self.cb = [Buf() for _ in range(8)]

            def __getitem__(self, k):
                kc = k[1]
                assert isinstance(kc, int)
                return V(self.ap[k], [self.cb[kc]])

        xg = Chunked("xg", F32)
        hT = Chunked("hT", BF16)
        KT = sb("KT", [128, 2, SEQ], BF16)
        Vt = sb("Vt", [128, 32, 256], BF16)
        IKT = sb("IKT", [128, SEQ], BF16)
        ring = [sb("ring%d" % i, [128, 4096], BF16) for i in range(NSLOT)]
        vecs = sb("vecs_s", [128, L, NV], F32)
        gqk = sb("gqk_s", [128, L, 256], F32)
        rope = sb("rope_s", [128, 32, 72], F32)
        cst = sb("cst_s", [128, 384 + NIT + 2], F32)
        ident_bf = sb("ident_bf", [128, 128], BF16)
        ones_bf = sb("ones_bf", [128, 128], BF16)
        epsc = sb("epsc", [128, 1], F32)
        gw_t = es.enter_context(nc.sbuf_tensor("gw_all", [128, 8, 30 + T], BF16))
        gw_bufs = [Buf() for _ in range(8)]
        gw_all = V(gw_t[:], gw_bufs)
        gw_ch = [V(gw_t[:][:, i, :], [gw_bufs[i]]) for i in range(8)]
        ones_f = sb("ones_f", [128, 128], F32)
        ident4 = sb("ident4", [128, 4, 128], BF16)
        cx_hist = sb("cx_hist", [128, 8, 2], F32)
        up_hist = sb("up_hist", [128, 44, 2], F32)
        attnT = sb("attnT", [128, 8, T], BF16)
        ARENA_KB = 78
        arena_t = es.enter_context(nc.sbuf_tensor("arena", [128, ARENA_KB * 512], BF16))
        arena_ap = arena_t[:]
        arena_atoms = [Atom() for _ in range(ARENA_KB)]
        ps_t = es.enter_context(nc.psum_tensor("ps", [128, 4096], F32))
        ps_ap = ps_t[:]
        bank_bufs = [Buf() for _ in range(8)]

        def bank(i, n=1):
            return V(ps_ap[:, i * 512:(i + n) * 512], bank_bufs[i:i + n])

        class Arena:
            def __init__(self):
                self.off = 0

            def reset(self):
                self.off = 0

            def alloc(self, shape, dt):
                nel = 1
                for s_ in shape[1:]:
                    nel *= s_
                nbytes = nel * (4 if dt == F32 else 2)
                kb = (nbytes + 1023) // 1024
                assert self.off + kb <= ARENA_KB, ("arena overflow", self.off, kb)
                ap = arena_ap[:, self.off * 512: self.off * 512 + nbytes // 2]
                if dt == F32:
                    ap = ap.bitcast(F32)
                if len(shape) == 3:
                    ap = ap.rearrange("p (a b) -> p a b", a=shape[1])
                elif len(shape) == 4:
                    ap = ap.rearrange("p (a b c) -> p a b c", a=shape[1], b=shape[2])
                v = V(ap, [Buf(arena_atoms[self.off:self.off + kb])])
                self.off += kb
                return v

        AR = Arena()

        sem_names = [("e", e) for e in Sched.ENGS] + [("d", k) for k in range(NDMA)]
        sems = {}
        for sk in sem_names:
            sems[sk] = es.enter_context(nc.semaphore("s_%s_%s" % sk))

        ident_f = cst[:, 0:128]
        m01 = cst[:, 128:256]
        negm = cst[:, 256:384]
        pow2 = cst[:, 384:384 + NIT + 2]

        B.dma(vecs, V(vecs_d, [Buf()]))
        B.dma(gqk, V(gqk_d, [Buf()]))
        B.dma(rope, V(rope_d, [Buf()]))
        B.dma(cst, V(cst_d, [Buf()]))
        B.copy("dve", ident_bf, ident_f)
        B.memset("dve", ones_bf, 1.0)
        B.memset("dve", ones_f, 1.0)
        for h4 in range(4):
            B.copy("dve", ident4[:, h4, :], ident_f)
        B.memset("dve", epsc, EPS)
        def convert(l, u):
            B.dma(wb_units[l][u], V(wst_d[l, u], wst_v.bufs), eng="pool")

        for u in range(NUNITS):
            convert(0, u)

        wstate = {"n": 0}

        def load_unit(l, u):
            slot = ring[wstate["n"] % NSLOT]
            wstate["n"] += 1
            B.dma(slot, wb_units[l][u])
            return slot

        class SlabStream:
            def __init__(self, l, u0):
                self.l = l
                self.u = u0
                self.i = 4
                self.cur = None

            def next(self):
                if self.i == 4:
                    self.cur = load_unit(self.l, self.u)
                    self.u += 1
                    self.i = 0
                v = self.cur[:, self.i * 1024:(self.i + 1) * 1024].rr("p (k j) -> p k j", k=8)
                self.i += 1
                return v

        def rms_sq(kc, sq):
            B.act(sq[kc % 2], xg[:, kc, :], AF.Square)
            B.mm(bank(7), ones_bf, sq[kc % 2], start=(kc == 0), stop=(kc == 7))

        def rms_fin(l, vcol, std, rstd):
            B.act(std, bank(7), AF.Sqrt, bias=epsc, scale=1.0 / D)
            B.recip(rstd, std)
            for kc in range(8):
                B.stt(hT[:, kc, :], xg[:, kc, :], vecs[:, l, vcol + kc:vcol + kc + 1], rstd, ALU.mult, ALU.mult)

        def rope_apply(dst, src, nh, half, tabA, tabB, tmp):
            ta = tabA.bc(1, [128, nh, 2 * half])
            tb = tabB.bc(1, [128, nh, 2 * half])
            x = src[:, :, 0:2 * half]
            t12 = tmp[0][:, 0:nh, 0:2 * half]
            t34 = tmp[1][:, 0:nh, 0:2 * half]
            B.tt("dve", t12, x, ta, ALU.mult)
            B.tt("dve", t34, x, tb, ALU.mult)
            B.tt("dve", dst[:, :, 0:half], t12[:, :, 0:half], t12[:, :, half:2 * half], ALU.subtract)
            B.tt("dve", dst[:, :, half:2 * half], t34[:, :, 0:half], t34[:, :, half:2 * half], ALU.add)

        prefetched = {"x": False}
        for l in range(L):
            src = xT_v if l == 0 else res_v[(l - 1) % 2]
            dst = yT_v if l == L - 1 else res_v[l % 2]
            B.memset("pool", gw_all, 0.0)
            B.memset("pool", cx_hist, 0.0)
            B.memset("pool", up_hist, 0.0)
            vl = lambda c0, n=1: vecs[:, l, c0:c0 + n]
            for gi in range(n_groups):
                t0 = gi * T
                if l + 1 < L:
                    for u in range(gi * 6, min((gi + 1) * 6, NUNITS)):
                        convert(l + 1, u)
                AR.reset()
                sq1 = [AR.alloc([128, T], BF16) for _ in range(2)]
                std1 = AR.alloc([128, T], F32)
                rstd1 = AR.alloc([128, T], F32)
                if not prefetched["x"]:
                    for kc in range(8):
                        B.dma(xg[:, kc, :], V(src[gi].ap[kc, :, t0:t0 + T], src[gi].bufs), eng="act")
                    for kc in range(8):
                        rms_sq(kc, sq1)
                rms_fin(l, V_N1, std1, rstd1)

                AR.reset()
                QT = AR.alloc([128, TT, 8, 128], BF16)
                IQT = AR.alloc([128, TT, 4, 128], BF16)
                wtok = AR.alloc([128, TT, 8], F32)
                p2_mark = AR.off
                NB2 = 3
                qb2 = [AR.alloc([128, 4, 128], BF16) for _ in range(NB2)]
                rt2 = [[AR.alloc([128, 8, 32], F32) for _ in range(2)] for _ in range(NB2)]
                sqf2 = [AR.alloc([128, 4, 128], BF16) for _ in range(NB2)]
                sml2 = [AR.alloc([128, 16], F32) for _ in range(NB2)]
                iqb2 = [AR.alloc([128, 8, 64], BF16) for _ in range(NB2)]
                ikb2 = [AR.alloc([128, 128], BF16) for _ in range(NB2)]
                p2_end = AR.off
                AR.off = p2_mark
                score = [AR.alloc([128, SEQ], F32) for _ in range(2)]
                maskb = [AR.alloc([128, SEQ], BF16) for _ in range(2)]
                Rr = [AR.alloc([128, 2, 512], BF16) for _ in range(2)]
                PT = [AR.alloc([128, 4, 128], BF16) for _ in range(4)]
                diag = [AR.alloc([128, 8, 128], BF16) for _ in range(2)]
                rden = AR.alloc([128, 8, 128], F32)
                smt_all = AR.alloc([128, 2, 32], F32)
                smt = [smt_all[:, 0, :], smt_all[:, 1, :]]
                AR.off = max(AR.off, p2_end)
                wus = {}
                chains = [(u, tt_) for u in range(5) for tt_ in range(TT)]

                def p2_proj(k):
                    u, tt_ = chains[k]
                    if tt_ == 0:
                        wus[u] = load_unit(l, u).rr("p (k j) -> p k j", k=8)
                    wu = wus[u]
                    ncol = 72 if u == 4 else 512
                    qi = gi * TT + tt_
                    par = k % NB2
                    pp = bank(k % 4)
                    qb, rt, sqf, iqb, ikb = qb2[par], rt2[par], sqf2[par], iqb2[par], ikb2[par]
                    ssq, stdq, rq = sml2[par][:, 0:4], sml2[par][:, 4:8], sml2[par][:, 8:12]
                    for kc in range(8):
                        B.mm(pp[:, 0:ncol], hT[:, kc, tt_ * 128:(tt_ + 1) * 128], wu[:, kc, 0:ncol],
                             start=(kc == 0), stop=(kc == 7))
                    tabA_a = rope[:, qi, 0:32]
                    tabB_a = rope[:, qi, 16:48]
                    tabA_i = rope[:, qi, 48:64]
                    tabB_i = rope[:, qi, 56:72]

                    def norm_rope_heads(nh, gcol):
                        B.act(sqf[:, 0:nh, :].rr("p h d -> p (h d)"), pp[:, 0:nh * 128], AF.Square)
                        B.reduce(ssq[:, 0:nh], sqf[:, 0:nh, :], ALU.add)
                        B.act(stdq[:, 0:nh], ssq[:, 0:nh], AF.Sqrt, bias=epsc, scale=1.0 / 128)
                        B.recip(rq[:, 0:nh], stdq[:, 0:nh])
                        for h in range(nh):
                            B.stt(qb[:, h, :], pp[:, h * 128:(h + 1) * 128], rq[:, h:h + 1],
                                  gqk[:, l, gcol:gcol + 128], ALU.mult, ALU.mult)
                        rope_apply(qb[:, 0:nh, :], qb[:, 0:nh, :], nh, 16, tabA_a, tabB_a, rt)

                    if u in (0, 1):
                        norm_rope_heads(4, 0)
                    elif u == 2:
                        norm_rope_heads(2, 128)
                        B.copy("act", Vt[:, qi, :], pp[:, 256:512])
                    elif u == 3:
                        ppv = pp.rr("p (h d) -> p h d", h=8)
                        rope_apply(iqb, ppv, 8, 8, tabA_i, tabB_i, rt)
                        B.copy("act", iqb[:, :, 16:64], ppv[:, :, 16:64])
                    else:
                        ppv = pp[:, 0:64].rr("p (h d) -> p h d", h=1)
                        ikv = ikb[:, 0:64].rr("p (h d) -> p h d", h=1)
                        rope_apply(ikv, ppv, 1, 8, tabA_i, tabB_i, rt)
                        B.copy("act", ikb[:, 16:64], pp[:, 16:64])
                        B.copy("act", ikb[:, 64:128], ikb[:, 0:64])
                        B.act(wtok[:, tt_, :], pp[:, 64:72], AF.Copy, scale=IW_SCALE)
                        if tt_ < 2:
                            B.tt("pool", diag[tt_], ident_f.bc(1, [128, 8, 128]), wtok[:, tt_, :].bc(2, [128, 8, 128]), ALU.mult)

                def p2_fin(k):
                    u, tt_ = chains[k]
                    qi = gi * TT + tt_
                    par = k % NB2
                    qb, iqb, ikb = qb2[par], iqb2[par], ikb2[par]
                    pT = bank(6 + k % 2).bitcast(BF16)
                    if u in (0, 1):
                        for h in range(4):
                            B.tr(pT[:, h * 128:(h + 1) * 128], qb[:, h, :], ident_bf)
                        B.copy("act", QT[:, tt_, 4 * u:4 * u + 4, :], pT[:, 0:512].rr("p (h t) -> p h t", h=4))
                    elif u == 2:
                        for h in range(2):
                            B.tr(pT[:, h * 128:(h + 1) * 128], qb[:, h, :], ident_bf)
                        B.copy("act", KT[:, :, qi * 128:(qi + 1) * 128], pT[:, 0:256].rr("p (h t) -> p h t", h=2))
                    elif u == 3:
                        for hp in range(4):
                            B.tr(pT[:, hp * 128:(hp + 1) * 128], iqb[:, 2 * hp:2 * hp + 2, :].rr("p h d -> p (h d)"), ident_bf)
                        B.copy("act", IQT[:, tt_, :, :], pT[:, 0:512].rr("p (h t) -> p h t", h=4))
                    else:
                        B.tr(pT[:, 0:128], ikb, ident_bf)
                        B.copy("act", IKT[:, qi * 128:(qi + 1) * 128], pT[:, 0:128])

                for k in range(len(chains)):
                    p2_proj(k)
                    if k >= 2:
                        p2_fin(k - 2)
                p2_fin(len(chains) - 2)
                p2_fin(len(chains) - 1)

                cnt_ix = {"r": 0, "e": 0}

                def idx(tt_):
                    qi = gi * TT + tt_
                    scur = (qi + 1) * 128
                    sc_ = score[tt_ % 2]
                    dg_ = diag[tt_ % 2]
                    if tt_ >= 2:
                        B.tt("pool", dg_, ident_f.bc(1, [128, 8, 128]), wtok[:, tt_, :].bc(2, [128, 8, 128]), ALU.mult)
                    nsc = (scur + 511) // 512
                    steps = [(sc, hp) for sc in range(nsc) for hp in range(4)]

                    def qk(p):
                        sc, hp = steps[p]
                        wdt = min(512, scur - sc * 512)
                        pr2 = bank(2 * (p % 2), 2).rr("p (a b) -> p a b", a=2)
                        for h2 in range(2):
                            p0 = h2 * 64
                            B.mm(pr2[:, h2, 0:wdt], IQT[p0:p0 + 64, tt_, hp, :], IKT[p0:p0 + 64, sc * 512:sc * 512 + wdt])
                        B.act(Rr[p % 2][:, :, 0:wdt], pr2[:, :, 0:wdt], AF.Relu)

                    def wsum(p):
                        sc, hp = steps[p]
                        wdt = min(512, scur - sc * 512)
                        pss = bank(4 + sc % 2)
                        for h2 in range(2):
                            h = 2 * hp + h2
                            B.mm(pss[:, 0:wdt], dg_[:, h, :], Rr[p % 2][:, h2, 0:wdt], start=(h == 0), stop=(h == 7))
                        if hp == 3:
                            B.copy("act", sc_[:, sc * 512:sc * 512 + wdt], pss[:, 0:wdt])

                    for p in range(len(steps)):
                        qk(p)
                        if p >= 1:
                            wsum(p - 1)
                    wsum(len(steps) - 1)

                def bis(tt_):
                    qi = gi * TT + tt_
                    scur = (qi + 1) * 128
                    sc_ = score[tt_ % 2]
                    mb_ = maskb[tt_ % 2]
                    sm = smt[tt_ % 2]
                    absmax = sm[:, 0:1]
                    mid = sm[:, 1:2]
                    cntc = sm[:, 2:3]
                    step = sm[:, 3:4]
                    tmpc = sm[:, 4:5]
                    thr = sm[:, 5:6]
                    halves = sm[:, 8:8 + NIT + 2]
                    dsl = slice(scur - 128, scur)
                    if qi >= 2:
                        B.reduce(absmax, sc_[:, 0:scur], ALU.max, absval=True)
                    B.tt("dve", sc_[:, dsl], sc_[:, dsl], m01, ALU.mult)
                    B.tt("dve", sc_[:, dsl], sc_[:, dsl], negm, ALU.add)
                    if qi >= 2:
                        B.ts("dve", halves, pow2, absmax, None, ALU.mult)
                        B.stt(mid, absmax, -1.001, halves[:, 0:1], ALU.mult, ALU.add)
                        for it in range(NIT):
                            B.ts("dve", mb_[:, 0:scur], sc_[:, 0:scur], mid, None, ALU.is_gt, ALU.add, accum=cntc)
                            B.ts("dve", step, cntc, TOPK - 0.5, halves[:, it:it + 1], ALU.is_gt, ALU.mult)
                            B.stt(mid, step, halves[:, it + 1:it + 2], mid, ALU.subtract, ALU.add)
                        B.ts("dve", thr, mid, halves[:, NIT:NIT + 1], None, ALU.subtract)
                    else:
                        B.memset("dve", thr, NEG / 2)
                    B.ts("dve", mb_[:, 0:scur], sc_[:, 0:scur], thr, -30000.0, ALU.is_le, ALU.mult)

                def att(tt_):
                    qi = gi * TT + tt_
                    nblk = qi + 1
                    tsl = slice(tt_ * 128, (tt_ + 1) * 128)
                    mb_ = maskb[tt_ % 2]
                    steps = [(j, g) for j in range(nblk) for g in range(2)]

                    def qkm(k):
                        j, g = steps[k]
                        psl = bank(4 + g)
                        B.mm(psl, KT[:, g, j * 128:(j + 1) * 128],
                             QT[:, tt_, 4 * g:4 * g + 4, :].rr("p h t -> p (h t)"), start=True, stop=False)
                        B.mm(psl, mb_[:, j * 128:(j + 1) * 128], ident4.rr("p h t -> p (h t)"), start=False, stop=True)
                        B.act(PT[k % 4].rr("p h t -> p (h t)"), psl, AF.Exp, scale=ATTN_SCALE)

                    def pvd(k):
                        j, g = steps[k]
                        ptf = PT[k % 4].rr("p h t -> p (h t)")
                        B.mm(bank(6 + g), Vt[:, j, g * 128:(g + 1) * 128], ptf, start=(j == 0), stop=(j == nblk - 1))
                        B.mm(bank(g), ones_bf, ptf, start=(j == 0), stop=(j == nblk - 1))

                    for k in range(len(steps)):
                        qkm(k)
                        if k >= 2:
                            pvd(k - 2)
                    pvd(len(steps) - 2)
                    pvd(len(steps) - 1)
                    B.copy("act", rden.rr("p h t -> p (h t)"), bank(0, 2))
                    B.recip(rden, rden)
                    B.tt("dve", attnT[:, :, tsl], bank(6, 2).rr("p (h t) -> p h t", h=8), rden, ALU.mult)

                idx(0)
                bis(0)
                for tt_ in range(1, TT):
                    idx(tt_)
                    bis(tt_)
                    att(tt_ - 1)
                att(TT - 1)

                AR.reset()
                accA = AR.alloc([128, 8, T], F32)
                aT = AR.alloc([128, 8, T], BF16)
                scT = AR.alloc([128, 8, T], BF16)
                cdiag = [AR.alloc([128, 31, 128], BF16) for _ in range(2)]
                sig = [AR.alloc([128, T], F32) for _ in range(3)]
                abf = [AR.alloc([128, T], BF16) for _ in range(2)]
                asq = [AR.alloc([128, T], BF16) for _ in range(2)]
                st_mean = AR.alloc([128, T], F32)
                st_b = AR.alloc([128, T], F32)
                st_rstd = AR.alloc([128, T], F32)
                tmpA = [AR.alloc([128, T], F32) for _ in range(2)]
                cxw = [AR.alloc([128, 2 + T], F32) for _ in range(2)]
                SS = SlabStream(l, 5)

                def build_cdiag(i):
                    B.tt("pool", cdiag[i % 2], ident_f.bc(1, [128, 31, 128]),
                         vl(V_CW + i * 31, 31).bc(2, [128, 31, 128]), ALU.mult)

                build_cdiag(0)
                build_cdiag(1)
                ps_sum = bank(6)
                ps_sq = bank(7)
                for i in range(8):
                    sv = SS.next()
                    sg_ = SS.next()
                    pv = bank((2 * i) % 4)
                    pg = bank((2 * i + 1) % 4)
                    pc = bank(4 + i % 2)
                    for kc in range(8):
                        B.mm(pv, sv[:, kc, :], hT[:, kc, :], start=(kc == 0), stop=(kc == 7))
                    for kc in range(8):
                        B.mm(pg, sg_[:, kc, :], hT[:, kc, :], start=(kc == 0), stop=(kc == 7))
                    gw = gw_ch[i]
                    B.act(sig[i % 2], pg, AF.Sigmoid)
                    B.tt("dve", gw[:, 30:30 + T], pv, sig[i % 2], ALU.mult)
                    cd = cdiag[i % 2]
                    for j in range(31):
                        B.mm(pc, cd[:, j, :], gw[:, j:j + T], start=(j == 0), stop=(j == 30))
                    if i + 2 < 8:
                        build_cdiag(i + 2)
                    B.act(accA[:, i, :], pc, AF.Identity, bias=vl(V_CB + i), scale=1.0)
                    B.act(abf[i % 2], pc, AF.Identity, bias=vl(V_CB + i), scale=1.0)
                    B.act(asq[i % 2], pc, AF.Square, bias=vl(V_CB + i), scale=1.0)
                    B.mm(ps_sum, ones_bf, abf[i % 2], start=(i == 0), stop=(i == 7))
                    B.mm(ps_sq, ones_bf, asq[i % 2], start=(i == 0), stop=(i == 7))
                B.copy("dve", gw_all[:, :, 0:30], gw_all[:, :, T:T + 30])
                B.act(st_mean, ps_sum, AF.Copy, scale=1.0 / D)
                B.tt("dve", st_b, st_mean, st_mean, ALU.mult)
                B.stt(st_b, ps_sq, 1.0 / D, st_b, ALU.mult, ALU.subtract)
                B.act(st_b, st_b, AF.Sqrt, bias=epsc, scale=1.0)
                B.recip(st_rstd, st_b)
                for i in range(8):
                    ta = tmpA[i % 2]
                    B.tt("dve", ta, accA[:, i, :], st_mean, ALU.subtract)
                    B.tt("dve", ta, ta, st_rstd, ALU.mult)
                    B.act(aT[:, i, :], ta, AF.Silu, bias=vl(V_LB + i), scale=vl(V_LG + i))
                for i in range(8):
                    sb_ = SS.next()
                    sc_ = SS.next()
                    sx_ = SS.next()
                    pb_ = bank((3 * i) % 6)
                    pc_ = bank((3 * i + 1) % 6)
                    px_ = bank((3 * i + 2) % 6)
                    for (pz, sz) in ((pb_, sb_), (pc_, sc_), (px_, sx_)):
                        for kc in range(8):
                            B.mm(pz, sz[:, kc, :], hT[:, kc, :], start=(kc == 0), stop=(kc == 7))
                    bsb = sig[0]
                    csb = sig[1]
                    cw_ = cxw[i % 2]
                    tacc = tmpA[i % 2]
                    B.copy("act", bsb, pb_)
                    B.copy("act", csb, pc_)
                    B.copy("act", cw_[:, 0:2], cx_hist[:, i, :])
                    B.tt("dve", cw_[:, 2:2 + T], px_, csb, ALU.mult)
                    B.copy("act", cx_hist[:, i, :], cw_[:, T:T + 2])
                    B.ts("dve", tacc, cw_[:, 0:T], vl(V_SW + i * 3), None, ALU.mult)
                    B.stt(tacc, cw_[:, 1:1 + T], vl(V_SW + i * 3 + 1), tacc, ALU.mult, ALU.add)
                    B.stt(tacc, cw_[:, 2:2 + T], vl(V_SW + i * 3 + 2), tacc, ALU.mult, ALU.add)
                    B.tt("dve", scT[:, i, :], tacc, bsb, ALU.mult)
                mergedT = accA.rr("p a b -> p (a b)").bitcast(BF16)[:, 0:8 * T].rr("p (a b) -> p a b", a=8)
                for n in range(8):
                    sl = [SS.next() for _ in range(6)]
                    rhs_of = [hT, hT, hT, aT, attnT, scT]
                    mb6 = [bank((6 * n + z) % 8) for z in range(6)]
                    for z in range(6):
                        pz = mb6[z]
                        for kc in range(8):
                            B.mm(pz, sl[z][:, kc, :], rhs_of[z][:, kc, :], start=(kc == 0), stop=(kc == 7))
                    for z in range(3):
                        B.act(sig[z], mb6[z], AF.Sigmoid)
                    m1 = tmpA[0]
                    m2 = tmpA[1]
                    B.tt("dve", m1, mb6[3], sig[0], ALU.mult)
                    B.tt("dve", m2, mb6[4], sig[1], ALU.mult)
                    B.tt("dve", m1, m1, m2, ALU.add)
                    B.tt("dve", m2, mb6[5], sig[2], ALU.mult)
                    B.tt("dve", mergedT[:, n, :], m1, m2, ALU.add)
                sq2v = sig[0].bitcast(BF16)
                sq2 = [sq2v[:, 0:T], sq2v[:, T:2 * T]]
                for n in range(8):
                    so = SS.next()
                    pz = bank(4 + n % 2)
                    for kc in range(8):
                        B.mm(pz, so[:, kc, :], mergedT[:, kc, :], start=(kc == 0), stop=(kc == 7))
                    B.tt("dve", xg[:, n, :], pz, xg[:, n, :], ALU.add)
                    rms_sq(n, sq2)

                rms_fin(l, V_N2, sig[1], sig[2])
                AR.reset()
                ffT_all = AR.alloc([128, 22, T], BF16)

                class _FF:
                    def __getitem__(self, k):
                        i = k[1]
                        return V(ffT_all.ap[k], [Buf(ffT_all.bufs[0].atoms[i:i + 1])])

                ffT = _FF()
                pbuf = [AR.alloc([128, 2 + T], F32) for _ in range(4)]
                facc = [AR.alloc([128, T], F32) for _ in range(4)]
                fsg = [AR.alloc([128, T], F32) for _ in range(2)]
                assert SS.u == 29 and SS.i == 4
                for i in range(22):
                    for z in range(2):
                        ch = i + 22 * z
                        sw_ = SS.next()
                        pz = bank((2 * i + z) % 8)
                        for kc in range(8):
                            B.mm(pz, sw_[:, kc, :], hT[:, kc, :], start=(kc == 0), stop=(kc == 7))
                        pb = pbuf[2 * (i % 2) + z]
                        fa = facc[2 * (i % 2) + z]
                        B.copy("act", pb[:, 0:2], up_hist[:, ch, :])
                        B.copy("act", pb[:, 2:2 + T], pz)
                        B.act(fa, pz, AF.Identity, bias=vl(V_FB + ch), scale=vl(V_FW + ch * 3 + 2))
                        B.copy("act", up_hist[:, ch, :], pb[:, T:T + 2])
                        B.stt(fa, pb[:, 0:T], vl(V_FW + ch * 3), fa, ALU.mult, ALU.add)
                        B.stt(fa, pb[:, 1:1 + T], vl(V_FW + ch * 3 + 1), fa, ALU.mult, ALU.add)
                    B.act(fsg[i % 2], facc[2 * (i % 2)], AF.Silu)
                    B.tt("pool", ffT[:, i, :], fsg[i % 2], facc[2 * (i % 2) + 1], ALU.mult)
                xo = [AR.alloc([128, T], F32) for _ in range(3)]
                sqn = [AR.alloc([128, T], BF16) for _ in range(2)]
                if gi + 1 < n_groups:
                    nxt = (src[gi + 1], (gi + 1) * T)
                elif l + 1 < L:
                    nxt = (res_v[l % 2][0], 0)
                else:
                    nxt = None
                for n in range(8):
                    pz = bank(4 + n % 2)
                    first = True
                    for kg in range(3):
                        sd = SS.next()
                        for kc in range(8):
                            k = kg * 8 + kc
                            if k >= 22:
                                break
                            B.mm(pz, sd[:, kc, :], ffT[:, k, :], start=first, stop=(k == 21))
                            first = False
                    B.tt("dve", xo[n % 3], pz, xg[:, n, :], ALU.add)
                    B.dma(V(dst[gi].ap[n, :, t0:t0 + T], dst[gi].bufs), xo[n % 3], eng="act")
                    if nxt is not None:
                        B.dma(xg[:, n, :], V(nxt[0].ap[n, :, nxt[1]:nxt[1] + T], nxt[0].bufs), eng="act")
                        rms_sq(n, sqn)
                prefetched["x"] = nxt is not None
                assert SS.u == NUNITS and SS.i == 4

        toks = [(("d", k), S.dma_val[k], "dma") for k in range(NDMA) if S.dma_val[k] > 0]
        S.final_wait("sp", toks)

        with nc.Block() as block:
            S.emit(nc, block, sems)
    return nc


def _rhs_unit(W, cols):
    u = np.zeros((128, 8, 512), np.float32)
    sub = W[:, cols]
    u[:, :, :len(cols)] = sub.reshape(8, 128, len(cols)).transpose(1, 0, 2)
    return u.reshape(128, 4096)


def _slab(Wk):
    nk = Wk.shape[0] // 128
    s = np.zeros((128, 8, 128), np.float32)
    s[:, :nk, :] = Wk.reshape(nk, 128, 128).transpose(1, 0, 2)
    return s


def _pack_layer(inp, l):
    w_in = inp["w_in"][l]
    units = []
    units.append(_rhs_unit(w_in, list(range(0, 512))))
    units.append(_rhs_unit(w_in, list(range(512, 1024))))
    units.append(_rhs_unit(w_in, list(range(1024, 1536))))
    units.append(_rhs_unit(w_in, list(range(1536, 2048))))
    units.append(_rhs_unit(w_in, list(range(2048, 2120))))
    slabs = []
    A0, C0, G0 = 2120, 4168, 7240
    for i in range(8):
        slabs.append(_slab(w_in[:, A0 + i * 128:A0 + (i + 1) * 128]))
        slabs.append(_slab(w_in[:, A0 + 1024 + i * 128:A0 + 1024 + (i + 1) * 128]))
    for i in range(8):
        for z in range(3):
            c0 = C0 + z * 1024 + i * 128
            slabs.append(_slab(w_in[:, c0:c0 + 128]))
    for n in range(8):
        for z in range(3):
            c0 = G0 + z * 1024 + n * 128
            slabs.append(_slab(w_in[:, c0:c0 + 128]))
        slabs.append(_slab(inp["w_conf_out"][l][:, n * 128:(n + 1) * 128]))
        slabs.append(_slab(inp["w_attn_out"][l][:, n * 128:(n + 1) * 128]))
        slabs.append(_slab(inp["w_sc_out"][l][:, n * 128:(n + 1) * 128]))
    for n in range(8):
        slabs.append(_slab(inp["w_o"][l][:, n * 128:(n + 1) * 128]))
    w_up = inp["w_up"][l]
    for i in range(22):
        slabs.append(_slab(w_up[:, i * 128:(i + 1) * 128]))
        slabs.append(_slab(w_up[:, FFN + i * 128:FFN + (i + 1) * 128]))
    w_dn = inp["w_down"][l]
    for n in range(8):
        for kg in range(3):
            k0, k1 = kg * 1024, min((kg + 1) * 1024, FFN)
            slabs.append(_slab(w_dn[k0:k1, n * 128:(n + 1) * 128]))
    assert len(slabs) == 41 * 4, len(slabs)
    for j in range(0, len(slabs), 4):
        units.append(np.stack(slabs[j:j + 4], axis=1).reshape(128, 4096))
    out = np.stack(units, axis=0)
    assert out.shape == (NUNITS, 128, 4096)
    return out


def _pack_vecs(inp, l):
    v = np.zeros((128, NV), np.float32)
    v[:, V_N1:V_N1 + 8] = inp["norm1_g"][l].reshape(8, 128).T
    v[:, V_N2:V_N2 + 8] = inp["norm2_g"][l].reshape(8, 128).T
    cw = inp["conf_conv_w"][l]
    v[:, V_CW:V_CW + 248] = cw.reshape(31, 8, 128).transpose(2, 1, 0).reshape(128, 248)
    v[:, V_CB:V_CB + 8] = inp["conf_conv_b"][l].reshape(8, 128).T
    v[:, V_LG:V_LG + 8] = inp["conf_ln_g"][l].reshape(8, 128).T
    v[:, V_LB:V_LB + 8] = inp["conf_ln_b"][l].reshape(8, 128).T
    v[:, V_SW:V_SW + 24] = inp["sc_conv_w"][l].reshape(3, 8, 128).transpose(2, 1, 0).reshape(128, 24)
    v[:, V_FW:V_FW + 132] = inp["ffn_conv_w"][l].reshape(3, 44, 128).transpose(2, 1, 0).reshape(128, 132)
    v[:, V_FB:V_FB + 44] = inp["ffn_conv_b"][l].reshape(44, 128).T
    return v


def _consts():
    c = np.zeros((128, 384 + NIT + 2), np.float32)
    c[:, 0:128] = np.eye(128, dtype=np.float32)
    tl = np.arange(128)
    valid = (tl[None, :] <= tl[:, None])
    c[:, 128:256] = valid.astype(np.float32)
    c[:, 256:384] = np.where(valid, 0.0, NEG).astype(np.float32)
    for i in range(NIT + 2):
        c[:, 384 + i] = 2.002 * (2.0 ** -(i + 1))
    return c


def _rope_tables():
    def tab(rot):
        pos = np.arange(SEQ, dtype=np.float32)
        inv = np.power(np.float32(500000.0), -np.arange(0, rot, 2, dtype=np.float32) / np.float32(rot)).astype(np.float32)
        ang = (pos[:, None] * inv[None, :]).astype(np.float32)
        return np.cos(ang).astype(np.float32), np.sin(ang).astype(np.float32)
    ca, sa = tab(32)
    ci, si = tab(16)
    r = np.concatenate([ca, sa, ca, ci, si, ci], axis=1)
    return np.ascontiguousarray(r.reshape(32, 128, 72).transpose(1, 0, 2))


_CACHE = {}


def _get_nc(n_layers, n_groups):
    key = (n_layers, n_groups)
    if key not in _CACHE:
        _CACHE[key] = build_program(n_layers, n_groups)
    return _CACHE[key]


def _shared_inputs(inp, layers):
    wst = np.stack([_pack_layer(inp, l) for l in layers], axis=0)
    vecs = np.ascontiguousarray(np.stack([_pack_vecs(inp, l) for l in layers], axis=1))
    gqk = np.zeros((128, len(layers), 256), np.float32)
    for i, l in enumerate(layers):
        gqk[:, i, 0:128] = inp["q_norm_g"][l][None, :]
        gqk[:, i, 128:256] = inp["k_norm_g"][l][None, :]
    return {"wst": wst, "vecs": vecs, "gqk": gqk, "rope": _rope_tables(), "cst": _consts()}


def run_layers(xT_list, inp, layers, n_groups=SEQ // T):
    nc = _get_nc(len(layers), n_groups)
    shared = _shared_inputs(inp, layers)
    in_maps = []
    for xT in xT_list:
        m = dict(shared)
        m["xT"] = xT
        in_maps.append(m)
    res = run_bass_kernel_spmd(nc, in_maps, core_ids=list(range(len(xT_list))))
    return [r["yT"] for r in res.results]


FUSED = True


def kernel(**inputs):
    inp = {k: np.asarray(v, dtype=np.float32) for k, v in inputs.items()}
    x = inp["x"]
    nb = x.shape[0]
    xT = [np.ascontiguousarray(x[b].T).reshape(8, 128, SEQ) for b in range(nb)]
    if FUSED:
        xT = run_layers(xT, inp, list(range(DEPTH)))
    else:
        for l in range(DEPTH):
            xT = run_layers(xT, inp, [l])
    out = np.stack([np.ascontiguousarray(t.reshape(D, SEQ).T) for t in xT], axis=0)
    return out.astype(np.float32)
```

```python
import numpy as np
import concourse.bass as bass
import concourse.mybir as mybir
from concourse.bass_utils import run_bass_kernel_spmd

F32 = mybir.dt.float32
BF16 = mybir.dt.bfloat16
ALU = mybir.AluOpType
AF = mybir.ActivationFunctionType
AX = mybir.AxisListType

D = 1024
SEQ = 4096
DEPTH = 4
NCORES = 8
T = 512
TT = 4
NUNITS = 46
NSLOT = 3
EPS = 1e-6
NEG = -1.0e4
NIT = 10
TOPK = 256
FFN = 2816
NV = 8 + 8 + 248 + 8 + 8 + 8 + 24 + 132 + 44
V_N1, V_N2, V_CW, V_CB, V_LG, V_LB, V_SW, V_FW, V_FB = 0, 8, 16, 264, 272, 280, 288, 312, 444
ATTN_SCALE = 128 ** -0.5
IW_SCALE = (8 ** -0.5) * (64 ** -0.5)


class Atom:
    __slots__ = ("w", "r")

    def __init__(self):
        self.w = None
        self.r = {}


class Buf:
    def __init__(self, atoms=None):
        self.atoms = atoms if atoms is not None else [Atom()]


class V:
    __slots__ = ("ap", "bufs")

    def __init__(self, ap, bufs):
        self.ap = ap
        self.bufs = bufs

    def __getitem__(self, k):
        return V(self.ap[k], self.bufs)

    def bitcast(self, dt):
        return V(self.ap.bitcast(dt), self.bufs)

    def rr(self, pat, **kw):
        return V(self.ap.rearrange(pat, **kw), self.bufs)

    def bc(self, axis, shape):
        return V(self.ap.unsqueeze(axis).broadcast_to(shape), self.bufs)


NDMA = 24


class Sched:
    ENGS = ("pe", "act", "dve", "pool", "sp")

    def __init__(self):
        self.ops = {e: [] for e in self.ENGS}
        self.cnt = {e: 0 for e in self.ENGS}
        self.waited = {e: {} for e in self.ENGS}
        self.dma_val = [0] * NDMA
        self.dma_next = {"sp": 0, "pool": 0, "act": 0}

    def op(self, eng, fn, reads=(), writes=(), dma=False):
        deps = {}

        def add(tok):
            if tok is None:
                return
            sk, val, teng = tok
            if teng == "pe" and eng == "pe" and not dma:
                return
            if deps.get(sk, 0) < val:
                deps[sk] = val

        for v in reads:
            for b in v.bufs:
                for a in b.atoms:
                    add(a.w)
        for v in writes:
            for b in v.bufs:
                for a in b.atoms:
                    add(a.w)
                    for sk, (val, teng) in a.r.items():
                        add((sk, val, teng))
        waits = []
        wd = self.waited[eng]
        for sk, val in deps.items():
            if wd.get(sk, 0) >= val:
                continue
            wd[sk] = val
            waits.append((sk, val))
        if dma:
            kk = self.dma_next[eng]
            self.dma_next[eng] = kk + 1
            if eng == "sp":
                k = kk % 12
            elif eng == "act":
                k = 12 + kk % 6
            else:
                k = 18 + kk % (NDMA - 18)
            prev = self.dma_val[k]
            sk = ("d", k)
            if prev > 0 and wd.get(sk, 0) < prev:
                wd[sk] = prev
                waits.append((sk, prev))
            self.dma_val[k] += 16
            tok = (sk, self.dma_val[k], "dma")
        else:
            self.cnt[eng] += 1
            tok = (("e", eng), self.cnt[eng], eng)
        self.ops[eng].append((waits, fn, tok, dma))
        for v in reads:
            for b in v.bufs:
                for a in b.atoms:
                    cur = a.r.get(tok[0])
                    if cur is None or cur[0] < tok[1]:
                        a.r[tok[0]] = (tok[1], tok[2])
        for v in writes:
            for b in v.bufs:
                for a in b.atoms:
                    a.w = tok
                    a.r = {}
        return tok

    def final_wait(self, eng, toks):
        waits = []
        for sk, val, _ in toks:
            waits.append((sk, val))
        self.ops[eng].append((waits, None, None, False))

    def emit(self, nc, block, sems):
        def semof(sk):
            return sems[sk]

        for eng, attr in (("pe", "tensor"), ("act", "scalar"), ("dve", "vector"), ("pool", "gpsimd"), ("sp", "sync")):
            ops = self.ops[eng]

            def body(e, ops=ops):
                for waits, fn, tok, dma in ops:
                    for sk, val in waits:
                        e.wait_ge(semof(sk), val)
                    if fn is None:
                        continue
                    ins = fn(e)
                    ins.then_inc(semof(tok[0]), 16 if dma else 1)

            getattr(block, attr)(body)


class Builder:
    def __init__(self, nc, n_layers, n_groups):
        self.nc = nc
        self.S = Sched()
        self.n_layers = n_layers
        self.n_groups = n_groups

    def mm(self, out, lhsT, rhs, start=True, stop=True):
        o, l, r = out.ap, lhsT.ap, rhs.ap
        self.S.op("pe", lambda e: e.matmul(o, l, r, start=start, stop=stop), reads=(lhsT, rhs), writes=(out,))

    def tr(self, out, in_, ident):
        o, i, d = out.ap, in_.ap, ident.ap
        self.S.op("pe", lambda e: e.transpose(o, i, d), reads=(in_, ident), writes=(out,))

    def act(self, out, in_, func, bias=None, scale=None, accum=None):
        kw = {}
        reads = [in_]
        if bias is not None:
            if isinstance(bias, V):
                kw["bias"] = bias.ap
                reads.append(bias)
            else:
                kw["bias"] = bias
        if scale is not None:
            if isinstance(scale, V):
                kw["scale"] = scale.ap
                reads.append(scale)
            else:
                kw["scale"] = scale
        writes = [out]
        if accum is not None:
            kw["accum_out"] = accum.ap
            writes.append(accum)
        o, i = out.ap, in_.ap
        self.S.op("act", lambda e: e.activation(o, i, func, **kw), reads=reads, writes=writes)

    def ts(self, eng, out, in0, s1, s2, op0, op1=None, accum=None):
        reads = [in0]
        a1 = s1
        if isinstance(s1, V):
            a1 = s1.ap
            reads.append(s1)
        a2 = s2
        if isinstance(s2, V):
            a2 = s2.ap
            reads.append(s2)
        writes = [out]
        kw = {}
        if accum is not None:
            kw["accum_out"] = accum.ap
            writes.append(accum)
        if op1 is not None:
            kw["op1"] = op1
        o, i = out.ap, in0.ap
        self.S.op(eng, lambda e: e.tensor_scalar(o, i, a1, a2, op0, **kw), reads=reads, writes=writes)

    def tt(self, eng, out, in0, in1, op):
        o, a, b = out.ap, in0.ap, in1.ap
        self.S.op(eng, lambda e: e.tensor_tensor(o, a, b, op), reads=(in0, in1), writes=(out,))

    def stt(self, out, in0, scalar, in1, op0, op1):
        reads = [in0, in1]
        sc = scalar
        if isinstance(scalar, V):
            sc = scalar.ap
            reads.append(scalar)
        o, a, b = out.ap, in0.ap, in1.ap
        self.S.op("dve", lambda e: e.scalar_tensor_tensor(o, a, sc, b, op0, op1), reads=reads, writes=(out,))

    def copy(self, eng, out, in_):
        o, i = out.ap, in_.ap
        if eng == "act":
            self.S.op("act", lambda e: e.copy(o, i), reads=(in_,), writes=(out,))
        else:
            self.S.op(eng, lambda e: e.tensor_copy(o, i), reads=(in_,), writes=(out,))

    def recip(self, out, in_):
        o, i = out.ap, in_.ap
        self.S.op("dve", lambda e: e.reciprocal(o, i), reads=(in_,), writes=(out,))

    def memset(self, eng, out, val):
        o = out.ap
        self.S.op(eng, lambda e: e.memset(o, val), writes=(out,))

    def reduce(self, out, in_, op, absval=False):
        o, i = out.ap, in_.ap
        kw = {"apply_absolute_value": True} if absval else {}
        self.S.op("dve", lambda e: e.tensor_reduce(o, i, AX.X, op, **kw), reads=(in_,), writes=(out,))

    def dma(self, out, in_, eng="sp"):
        o, i = out.ap, in_.ap
        return self.S.op(eng, lambda e: e.dma_start(out=o, in_=i), reads=(in_,), writes=(out,), dma=True)


def build_program(n_layers=DEPTH, n_groups=SEQ // T):
    from contextlib import ExitStack

    nc = bass.Bass("TRN2", target_bir_lowering=False)
    B = Builder(nc, n_layers, n_groups)
    S = B.S
    L = n_layers

    def dram(name, shape, dt, kind):
        return nc.dram_tensor(name, shape, dt, kind=kind).ap()

    xT_d = dram("xT", [8, 128, SEQ], F32, "ExternalInput")
    wst_d = dram("wst", [L, NUNITS, 128, 4096], F32, "ExternalInput")
    vecs_d = dram("vecs", [128, L, NV], F32, "ExternalInput")
    gqk_d = dram("gqk", [128, L, 256], F32, "ExternalInput")
    rope_d = dram("rope", [128, 32, 72], F32, "ExternalInput")
    cst_d = dram("cst", [128, 384 + NIT + 2], F32, "ExternalInput")
    yT_d = dram("yT", [8, 128, SEQ], F32, "ExternalOutput")
    wb_d = dram("wb", [L, NUNITS, 128, 4096], BF16, "Internal")
    res_d = [dram("res%d" % i, [8, 128, SEQ], F32, "Internal") for i in range(2)]

    xT_v = [V(xT_d, [Buf()]) for _ in range(SEQ // T)]
    yT_v = [V(yT_d, [Buf()]) for _ in range(SEQ // T)]
    res_v = [[V(r, [Buf()]) for _ in range(SEQ // T)] for r in res_d]
    wst_v = V(wst_d, [Buf()])
    wb_units = [[V(wb_d[l, u], [Buf()]) for u in range(NUNITS)] for l in range(L)]

    es = ExitStack()
    with es:
        def sb(name, shape, dt):
            t = es.enter_context(nc.sbuf_tensor(name, shape, dt))
            return V(t[:], [Buf()])

        class Chunked:
            def __init__(self, name, dt):
                t = es.enter_context(nc.sbuf_tensor(name, [128, 8, T], dt))
                self.ap = t[:]
                self.cb = [Buf() for _ in range(8)]

            def __getitem__(self, k):
                kc = k[1]
                assert isinstance(kc, int)
                return V(self.ap[k], [self.cb[kc]])

        xg = Chunked("xg", F32)
        hT = Chunked("hT", BF16)
        KT = sb("KT", [128, 2, SEQ], BF16)
        Vt = sb("Vt", [128, 32, 256], BF16)
        IKT = sb("IKT", [128, SEQ], BF16)
        ring = [sb("ring%d" % i, [128, 4096], BF16) for i in range(NSLOT)]
        vecs = sb("vecs_s", [128, L, NV], F32)
        gqk = sb("gqk_s", [128, L, 256], F32)
        rope = sb("rope_s", [128, 32, 72], F32)
        cst = sb("cst_s", [128, 384 + NIT + 2], F32)
        ident_bf = sb("ident_bf", [128, 128], BF16)
        ones_bf = sb("ones_bf", [128, 128], BF16)
        epsc = sb("epsc", [128, 1], F32)
        gw_t = es.enter_context(nc.sbuf_tensor("gw_all", [128, 8, 30 + T], BF16))
        gw_bufs = [Buf() for _ in range(8)]
        gw_all = V(gw_t[:], gw_bufs)
        gw_ch = [V(gw_t[:][:, i, :], [gw_bufs[i]]) for i in range(8)]
        ones_f = sb("ones_f", [128, 128], F32)
        ident4 = sb("ident4", [128, 4, 128], BF16)
        cx_hist = sb("cx_hist", [128, 8, 2], F32)
        up_hist = sb("up_hist", [128, 44, 2], F32)
        attnT = sb("attnT", [128, 8, T], BF16)
        ARENA_KB = 78
        arena_t = es.enter_context(nc.sbuf_tensor("arena", [128, ARENA_KB * 512], BF16))
        arena_ap = arena_t[:]
        arena_atoms = [Atom() for _ in range(ARENA_KB)]
        ps_t = es.enter_context(nc.psum_tensor("ps", [128, 4096], F32))
        ps_ap = ps_t[:]
        bank_bufs = [Buf() for _ in range(8)]

        def bank(i, n=1):
            return V(ps_ap[:, i * 512:(i + n) * 512], bank_bufs[i:i + n])

        class Arena:
            def __init__(self):
                self.off = 0

            def reset(self):
                self.off = 0

            def alloc(self, shape, dt):
                nel = 1
                for s_ in shape[1:]:
                    nel *= s_
                nbytes = nel * (4 if dt == F32 else 2)
                kb = (nbytes + 1023) // 1024
                assert self.off + kb <= ARENA_KB, ("arena overflow", self.off, kb)
                ap = arena_ap[:, self.off * 512: self.off * 512 + nbytes // 2]
                if dt == F32:
                    ap = ap.bitcast(F32)
                if len(shape) == 3:
                    ap = ap.rearrange("p (a b) -> p a b", a=shape[1])
                elif len(shape) == 4:
                    ap = ap.rearrange("p (a b c) -> p a b c", a=shape[1], b=shape[2])
                v = V(ap, [Buf(arena_atoms[self.off:self.off + kb])])
                self.off += kb
                return v

        AR = Arena()

        sem_names = [("e", e) for e in Sched.ENGS] + [("d", k) for k in range(NDMA)]
        sems = {}
        for sk in sem_names:
            sems[sk] = es.enter_context(nc.semaphore("s_%s_%s" % sk))

        ident_f = cst[:, 0:128]
        m01 = cst[:, 128:256]
        negm = cst[:, 256:384]
        pow2 = cst[:, 384:384 + NIT + 2]

        B.dma(vecs, V(vecs_d, [Buf()]))
        B.dma(gqk, V(gqk_d, [Buf()]))
        B.dma(rope, V(rope_d, [Buf()]))
        B.dma(cst, V(cst_d, [Buf()]))
        B.copy("dve", ident_bf, ident_f)
        B.memset("dve", ones_bf, 1.0)
        B.memset("dve", ones_f, 1.0)
        for h4 in range(4):
            B.copy("dve", ident4[:, h4, :], ident_f)
        B.memset("dve", epsc, EPS)
        def convert(l, u):
            B.dma(wb_units[l][u], V(wst_d[l, u], wst_v.bufs), eng="pool")

        for u in range(NUNITS):
            convert(0, u)

        wstate = {"n": 0}

        def load_unit(l, u):
            slot = ring[wstate["n"] % NSLOT]
            wstate["n"] += 1
            B.dma(slot, wb_units[l][u])
            return slot

        class SlabStream:
            def __init__(self, l, u0):
                self.l = l
                self.u = u0
                self.i = 4
                self.cur = None

            def next(self):
                if self.i == 4:
                    self.cur = load_unit(self.l, self.u)
                    self.u += 1
                    self.i = 0
                v = self.cur[:, self.i * 1024:(self.i + 1) * 1024].rr("p (k j) -> p k j", k=8)
                self.i += 1
                return v

        def rms_sq(kc, sq):
            B.act(sq[kc % 2], xg[:, kc, :], AF.Square)
            B.mm(bank(7), ones_bf, sq[kc % 2], start=(kc == 0), stop=(kc == 7))

        def rms_fin(l, vcol, std, rstd):
            B.act(std, bank(7), AF.Sqrt, bias=epsc, scale=1.0 / D)
            B.recip(rstd, std)
            for kc in range(8):
                B.stt(hT[:, kc, :], xg[:, kc, :], vecs[:, l, vcol + kc:vcol + kc + 1], rstd, ALU.mult, ALU.mult)

        def rope_apply(dst, src, nh, half, tabA, tabB, tmp):
            ta = tabA.bc(1, [128, nh, 2 * half])
            tb = tabB.bc(1, [128, nh, 2 * half])
            x = src[:, :, 0:2 * half]
            t12 = tmp[0][:, 0:nh, 0:2 * half]
            t34 = tmp[1][:, 0:nh, 0:2 * half]
            B.tt("dve", t12, x, ta, ALU.mult)
            B.tt("dve", t34, x, tb, ALU.mult)
            B.tt("dve", dst[:, :, 0:half], t12[:, :, 0:half], t12[:, :, half:2 * half], ALU.subtract)
            B.tt("dve", dst[:, :, half:2 * half], t34[:, :, 0:half], t34[:, :, half:2 * half], ALU.add)

        prefetched = {"x": False}
        for l in range(L):
            src = xT_v if l == 0 else res_v[(l - 1) % 2]
            dst = yT_v if l == L - 1 else res_v[l % 2]
            B.memset("pool", gw_all, 0.0)
            B.memset("pool", cx_hist, 0.0)
            B.memset("pool", up_hist, 0.0)
            vl = lambda c0, n=1: vecs[:, l, c0:c0 + n]
            for gi in range(n_groups):
                t0 = gi * T
                if l + 1 < L:
                    for u in range(gi * 6, min((gi + 1) * 6, NUNITS)):
                        convert(l + 1, u)
                AR.reset()
                sq1 = [AR.alloc([128, T], BF16) for _ in range(2)]
                std1 = AR.alloc([128, T], F32)
                rstd1 = AR.alloc([128, T], F32)
                if not prefetched["x"]:
                    for kc in range(8):
                        B.dma(xg[:, kc, :], V(src[gi].ap[kc, :, t0:t0 + T], src[gi].bufs), eng="act")
                for kc in range(8):
                    rms_sq(kc, sq1)
                rms_fin(l, V_N1, std1, rstd1)

                AR.reset()
                QT = AR.alloc([128, TT, 8, 128], BF16)
                IQT = AR.alloc([128, TT, 4, 128], BF16)
                wtok = AR.alloc([128, TT, 8], F32)
                p2_mark = AR.off
                NB2 = 3
                qb2 = [AR.alloc([128, 4, 128], BF16) for _ in range(NB2)]
                rt2 = [[AR.alloc([128, 8, 32], F32) for _ in range(2)] for _ in range(NB2)]
                sqf2 = [AR.alloc([128, 4, 128], BF16) for _ in range(NB2)]
                sml2 = [AR.alloc([128, 16], F32) for _ in range(NB2)]
                iqb2 = [AR.alloc([128, 8, 64], BF16) for _ in range(NB2)]
                ikb2 = [AR.alloc([128, 128], BF16) for _ in range(NB2)]
                p2_end = AR.off
                AR.off = p2_mark
                score = [AR.alloc([128, SEQ], F32) for _ in range(2)]
                maskb = [AR.alloc([128, SEQ], BF16) for _ in range(2)]
                Rr = [AR.alloc([128, 2, 512], BF16) for _ in range(2)]
                PT = [AR.alloc([128, 4, 128], BF16) for _ in range(4)]
                diag = [AR.alloc([128, 8, 128], BF16) for _ in range(2)]
                rden = AR.alloc([128, 8, 128], F32)
                smt_all = AR.alloc([128, 2, 32], F32)
                smt = [smt_all[:, 0, :], smt_all[:, 1, :]]
                AR.off = max(AR.off, p2_end)
                wus = {}
                chains = [(u, tt_) for u in range(5) for tt_ in range(TT)]

                def p2_proj(k):
                    u, tt_ = chains[k]
                    if tt_ == 0:
                        wus[u] = load_unit(l, u).rr("p (k j) -> p k j", k=8)
                    wu = wus[u]
                    ncol = 72 if u == 4 else 512
                    qi = gi * TT + tt_
                    par = k % NB2
                    pp = bank(k % 4)
                    qb, rt, sqf, iqb, ikb = qb2[par], rt2[par], sqf2[par], iqb2[par], ikb2[par]
                    ssq, stdq, rq = sml2[par][:, 0:4], sml2[par][:, 4:8], sml2[par][:, 8:12]
                    for kc in range(8):
                        B.mm(pp[:, 0:ncol], hT[:, kc, tt_ * 128:(tt_ + 1) * 128], wu[:, kc, 0:ncol],
                             start=(kc == 0), stop=(kc == 7))
                    tabA_a = rope[:, qi, 0:32]
                    tabB_a = rope[:, qi, 16:48]
                    tabA_i = rope[:, qi, 48:64]
                    tabB_i = rope[:, qi, 56:72]

                    def norm_rope_heads(nh, gcol):
                        B.act(sqf[:, 0:nh, :].rr("p h d -> p (h d)"), pp[:, 0:nh * 128], AF.Square)
                        B.reduce(ssq[:, 0:nh], sqf[:, 0:nh, :], ALU.add)
                        B.act(stdq[:, 0:nh], ssq[:, 0:nh], AF.Sqrt, bias=epsc, scale=1.0 / 128)
                        B.recip(rq[:, 0:nh], stdq[:, 0:nh])
                        for h in range(nh):
                            B.stt(qb[:, h, :], pp[:, h * 128:(h + 1) * 128], rq[:, h:h + 1],
                                  gqk[:, l, gcol:gcol + 128], ALU.mult, ALU.mult)
                        rope_apply(qb[:, 0:nh, :], qb[:, 0:nh, :], nh, 16, tabA_a, tabB_a, rt)

                    if u in (0, 1):
                        norm_rope_heads(4, 0)
                    elif u == 2:
                        norm_rope_heads(2, 128)
                        B.copy("act", Vt[:, qi, :], pp[:, 256:512])
                    elif u == 3:
                        ppv = pp.rr("p (h d) -> p h d", h=8)
                        rope_apply(iqb, ppv, 8, 8, tabA_i, tabB_i, rt)
                        B.copy("act", iqb[:, :, 16:64], ppv[:, :, 16:64])
                    else:
                        ppv = pp[:, 0:64].rr("p (h d) -> p h d", h=1)
                        ikv = ikb[:, 0:64].rr("p (h d) -> p h d", h=1)
                        rope_apply(ikv, ppv, 1, 8, tabA_i, tabB_i, rt)
                        B.copy("act", ikb[:, 16:64], pp[:, 16:64])
                        B.copy("act", ikb[:, 64:128], ikb[:, 0:64])
                        B.act(wtok[:, tt_, :], pp[:, 64:72], AF.Copy, scale=IW_SCALE)
                        if tt_ < 2:
                            B.tt("pool", diag[tt_], ident_f.bc(1, [128, 8, 128]), wtok[:, tt_, :].bc(2, [128, 8, 128]), ALU.mult)

                def p2_fin(k):
                    u, tt_ = chains[k]
                    qi = gi * TT + tt_
                    par = k % NB2
                    qb, iqb, ikb = qb2[par], iqb2[par], ikb2[par]
                    pT = bank(6 + k % 2).bitcast(BF16)
                    if u in (0, 1):
                        for h in range(4):
                            B.tr(pT[:, h * 128:(h + 1) * 128], qb[:, h, :], ident_bf)
                        B.copy("act", QT[:, tt_, 4 * u:4 * u + 4, :], pT[:, 0:512].rr("p (h t) -> p h t", h=4))
                    elif u == 2:
                        for h in range(2):
                            B.tr(pT[:, h * 128:(h + 1) * 128], qb[:, h, :], ident_bf)
                        B.copy("act", KT[:, :, qi * 128:(qi + 1) * 128], pT[:, 0:256].rr("p (h t) -> p h t", h=2))
                    elif u == 3:
                        for hp in range(4):
                            B.tr(pT[:, hp * 128:(hp + 1) * 128], iqb[:, 2 * hp:2 * hp + 2, :].rr("p h d -> p (h d)"), ident_bf)
                        B.copy("act", IQT[:, tt_, :, :], pT[:, 0:512].rr("p (h t) -> p h t", h=4))
                    else:
                        B.tr(pT[:, 0:128], ikb, ident_bf)
                        B.copy("act", IKT[:, qi * 128:(qi + 1) * 128], pT[:, 0:128])

                for k in range(len(chains)):
                    p2_proj(k)
                    if k >= 2:
                        p2_fin(k - 2)
                p2_fin(len(chains) - 2)
                p2_fin(len(chains) - 1)

                cnt_ix = {"r": 0, "e": 0}

                def idx(tt_):
                    qi = gi * TT + tt_
                    scur = (qi + 1) * 128
                    sc_ = score[tt_ % 2]
                    dg_ = diag[tt_ % 2]
                    if tt_ >= 2:
                        B.tt("pool", dg_, ident_f.bc(1, [128, 8, 128]), wtok[:, tt_, :].bc(2, [128, 8, 128]), ALU.mult)
                    nsc = (scur + 511) // 512
                    steps = [(sc, hp) for sc in range(nsc) for hp in range(4)]

                    def qk(p):
                        sc, hp = steps[p]
                        wdt = min(512, scur - sc * 512)
                        pr2 = bank(2 * (p % 2), 2).rr("p (a b) -> p a b", a=2)
                        for h2 in range(2):
                            p0 = h2 * 64
                            B.mm(pr2[:, h2, 0:wdt], IQT[p0:p0 + 64, tt_, hp, :], IKT[p0:p0 + 64, sc * 512:sc * 512 + wdt])
                        B.act(Rr[p % 2][:, :, 0:wdt], pr2[:, :, 0:wdt], AF.Relu)

                    def wsum(p):
                        sc, hp = steps[p]
                        wdt = min(512, scur - sc * 512)
                        pss = bank(4 + sc % 2)
                        for h2 in range(2):
                            h = 2 * hp + h2
                            B.mm(pss[:, 0:wdt], dg_[:, h, :], Rr[p % 2][:, h2, 0:wdt], start=(h == 0), stop=(h == 7))
                        if hp == 3:
                            B.copy("act", sc_[:, sc * 512:sc * 512 + wdt], pss[:, 0:wdt])

                    for p in range(len(steps)):
                        qk(p)
                        if p >= 1:
                            wsum(p - 1)
                    wsum(len(steps) - 1)

                def bis(tt_):
                    qi = gi * TT + tt_
                    scur = (qi + 1) * 128
                    sc_ = score[tt_ % 2]
                    mb_ = maskb[tt_ % 2]
                    sm = smt[tt_ % 2]
                    absmax = sm[:, 0:1]
                    mid = sm[:, 1:2]
                    cntc = sm[:, 2:3]
                    step = sm[:, 3:4]
                    tmpc = sm[:, 4:5]
                    thr = sm[:, 5:6]
                    halves = sm[:, 8:8 + NIT + 2]
                    dsl = slice(scur - 128, scur)
                    if qi >= 2:
                        B.reduce(absmax, sc_[:, 0:scur], ALU.max, absval=True)
                    B.tt("dve", sc_[:, dsl], sc_[:, dsl], m01, ALU.mult)
                    B.tt("dve", sc_[:, dsl], sc_[:, dsl], negm, ALU.add)
                    if qi >= 2:
                        B.ts("dve", halves, pow2, absmax, None, ALU.mult)
                        B.stt(mid, absmax, -1.001, halves[:, 0:1], ALU.mult, ALU.add)
                        for it in range(NIT):
                            B.ts("dve", mb_[:, 0:scur], sc_[:, 0:scur], mid, None, ALU.is_gt, ALU.add, accum=cntc)
                            B.ts("dve", step, cntc, TOPK - 0.5, halves[:, it:it + 1], ALU.is_gt, ALU.mult)
                            B.stt(mid, step, halves[:, it + 1:it + 2], mid, ALU.subtract, ALU.add)
                        B.ts("dve", thr, mid, halves[:, NIT:NIT + 1], None, ALU.subtract)
                    else:
                        B.memset("dve", thr, NEG / 2)
                    B.ts("dve", mb_[:, 0:scur], sc_[:, 0:scur], thr, -30000.0, ALU.is_le, ALU.mult)

                def att(tt_):
                    qi = gi * TT + tt_
                    nblk = qi + 1
                    tsl = slice(tt_ * 128, (tt_ + 1) * 128)
                    mb_ = maskb[tt_ % 2]
                    steps = [(j, g) for j in range(nblk) for g in range(2)]

                    def qkm(k):
                        j, g = steps[k]
                        psl = bank(4 + g)
                        B.mm(psl, KT[:, g, j * 128:(j + 1) * 128],
                             QT[:, tt_, 4 * g:4 * g + 4, :].rr("p h t -> p (h t)"), start=True, stop=False)
                        B.mm(psl, mb_[:, j * 128:(j + 1) * 128], ident4.rr("p h t -> p (h t)"), start=False, stop=True)
                        B.act(PT[k % 4].rr("p h t -> p (h t)"), psl, AF.Exp, scale=ATTN_SCALE)

                    def pvd(k):
                        j, g = steps[k]
                        ptf = PT[k % 4].rr("p h t -> p (h t)")
                        B.mm(bank(6 + g), Vt[:, j, g * 128:(g + 1) * 128], ptf, start=(j == 0), stop=(j == nblk - 1))
                        B.mm(bank(g), ones_bf, ptf, start=(j == 0), stop=(j == nblk - 1))

                    for k in range(len(steps)):
                        qkm(k)
                        if k >= 2:
                            pvd(k - 2)
                    pvd(len(steps) - 2)
                    pvd(len(steps) - 1)
                    B.copy("act", rden.rr("p h t -> p (h t)"), bank(0, 2))
                    B.recip(rden, rden)
                    B.tt("dve", attnT[:, :, tsl], bank(6, 2).rr("p (h t) -> p h t", h=8), rden, ALU.mult)

                idx(0)
                bis(0)
                for tt_ in range(1, TT):
                    idx(tt_)
                    bis(tt_)
                    att(tt_ - 1)
                att(TT - 1)

                AR.reset()
                accA = AR.alloc([128, 8, T], F32)
                aT = AR.alloc([128, 8, T], BF16)
                scT = AR.alloc([128, 8, T], BF16)
                cdiag = [AR.alloc([128, 31, 128], BF16) for _ in range(2)]
                sig = [AR.alloc([128, T], F32) for _ in range(3)]
                abf = [AR.alloc([128, T], BF16) for _ in range(2)]
                asq = [AR.alloc([128, T], BF16) for _ in range(2)]
                st_mean = AR.alloc([128, T], F32)
                st_b = AR.alloc([128, T], F32)
                st_rstd = AR.alloc([128, T], F32)
                tmpA = [AR.alloc([128, T], F32) for _ in range(2)]
                cxw = [AR.alloc([128, 2 + T], F32) for _ in range(2)]
                SS = SlabStream(l, 5)

                def build_cdiag(i):
                    B.tt("pool", cdiag[i % 2], ident_f.bc(1, [128, 31, 128]),
                         vl(V_CW + i * 31, 31).bc(2, [128, 31, 128]), ALU.mult)

                build_cdiag(0)
                build_cdiag(1)
                ps_sum = bank(6)
                ps_sq = bank(7)
                for i in range(8):
                    sv = SS.next()
                    sg_ = SS.next()
                    pv = bank((2 * i) % 4)
                    pg = bank((2 * i + 1) % 4)
                    pc = bank(4 + i % 2)
                    for kc in range(8):
                        B.mm(pv, sv[:, kc, :], hT[:, kc, :], start=(kc == 0), stop=(kc == 7))
                    for kc in range(8):
                        B.mm(pg, sg_[:, kc, :], hT[:, kc, :], start=(kc == 0), stop=(kc == 7))
                    gw = gw_ch[i]
                    B.act(sig[i % 2], pg, AF.Sigmoid)
                    B.tt("dve", gw[:, 30:30 + T], pv, sig[i % 2], ALU.mult)
                    cd = cdiag[i % 2]
                    for j in range(31):
                        B.mm(pc, cd[:, j, :], gw[:, j:j + T], start=(j == 0), stop=(j == 30))
                    if i + 2 < 8:
                        build_cdiag(i + 2)
                    B.act(accA[:, i, :], pc, AF.Identity, bias=vl(V_CB + i), scale=1.0)
                    B.act(abf[i % 2], pc, AF.Identity, bias=vl(V_CB + i), scale=1.0)
                    B.act(asq[i % 2], pc, AF.Square, bias=vl(V_CB + i), scale=1.0)
                    B.mm(ps_sum, ones_bf, abf[i % 2], start=(i == 0), stop=(i == 7))
                    B.mm(ps_sq, ones_bf, asq[i % 2], start=(i == 0), stop=(i == 7))
                B.copy("dve", gw_all[:, :, 0:30], gw_all[:, :, T:T + 30])
                B.act(st_mean, ps_sum, AF.Copy, scale=1.0 / D)
                B.tt("dve", st_b, st_mean, st_mean, ALU.mult)
                B.stt(st_b, ps_sq, 1.0 / D, st_b, ALU.mult, ALU.subtract)
                B.act(st_b, st_b, AF.Sqrt, bias=epsc, scale=1.0)
                B.recip(st_rstd, st_b)
                for i in range(8):
                    ta = tmpA[i % 2]
                    B.tt("dve", ta, accA[:, i, :], st_mean, ALU.subtract)
                    B.tt("dve", ta, ta, st_rstd, ALU.mult)
                    B.act(aT[:, i, :], ta, AF.Silu, bias=vl(V_LB + i), scale=vl(V_LG + i))
                for i in range(8):
                    sb_ = SS.next()
                    sc_ = SS.next()
                    sx_ = SS.next()
                    pb_ = bank((3 * i) % 6)
                    pc_ = bank((3 * i + 1) % 6)
                    px_ = bank((3 * i + 2) % 6)
                    for (pz, sz) in ((pb_, sb_), (pc_, sc_), (px_, sx_)):
                        for kc in range(8):
                            B.mm(pz, sz[:, kc, :], hT[:, kc, :], start=(kc == 0), stop=(kc == 7))
                    bsb = sig[0]
                    csb = sig[1]
                    cw_ = cxw[i % 2]
                    tacc = tmpA[i % 2]
                    B.copy("act", bsb, pb_)
                    B.copy("act", csb, pc_)
                    B.copy("act", cw_[:, 0:2], cx_hist[:, i, :])
                    B.tt("dve", cw_[:, 2:2 + T], px_, csb, ALU.mult)
                    B.copy("act", cx_hist[:, i, :], cw_[:, T:T + 2])
                    B.ts("dve", tacc, cw_[:, 0:T], vl(V_SW + i * 3), None, ALU.mult)
                    B.stt(tacc, cw_[:, 1:1 + T], vl(V_SW + i * 3 + 1), tacc, ALU.mult, ALU.add)
                    B.stt(tacc, cw_[:, 2:2 + T], vl(V_SW + i * 3 + 2), tacc, ALU.mult, ALU.add)
                    B.tt("dve", scT[:, i, :], tacc, bsb, ALU.mult)
                mergedT = accA.rr("p a b -> p (a b)").bitcast(BF16)[:, 0:8 * T].rr("p (a b) -> p a b", a=8)
                for n in range(8):
                    sl = [SS.next() for _ in range(6)]
                    rhs_of = [hT, hT, hT, aT, attnT, scT]
                    mb6 = [bank((6 * n + z) % 8) for z in range(6)]
                    for z in range(6):
                        pz = mb6[z]
                        for kc in range(8):
                            B.mm(pz, sl[z][:, kc, :], rhs_of[z][:, kc, :], start=(kc == 0), stop=(kc == 7))
                    for z in range(3):
                        B.act(sig[z], mb6[z], AF.Sigmoid)
                    m1 = tmpA[0]
                    m2 = tmpA[1]
                    B.tt("dve", m1, mb6[3], sig[0], ALU.mult)
                    B.tt("dve", m2, mb6[4], sig[1], ALU.mult)
                    B.tt("dve", m1, m1, m2, ALU.add)
                    B.tt("dve", m2, mb6[5], sig[2], ALU.mult)
                    B.tt("dve", mergedT[:, n, :], m1, m2, ALU.add)
                sq2v = sig[0].bitcast(BF16)
                sq2 = [sq2v[:, 0:T], sq2v[:, T:2 * T]]
                for n in range(8):
                    so = SS.next()
                    pz = bank(4 + n % 2)
                    for kc in range(8):
                        B.mm(pz, so[:, kc, :], mergedT[:, kc, :], start=(kc == 0), stop=(kc == 7))
                    B.tt("dve", xg[:, n, :], pz, xg[:, n, :], ALU.add)
                    rms_sq(n, sq2)

                rms_fin(l, V_N2, sig[1], sig[2])
                AR.reset()
                ffT_all = AR.alloc([128, 22, T], BF16)

                class _FF:
                    def __getitem__(self, k):
                        i = k[1]
                        return V(ffT_all.ap[k], [Buf(ffT_all.bufs[0].atoms[i:i + 1])])

                ffT = _FF()
                pbuf = [AR.alloc([128, 2 + T], F32) for _ in range(4)]
                facc = [AR.alloc([128, T], F32) for _ in range(4)]
                fsg = [AR.alloc([128, T], F32) for _ in range(2)]
                assert SS.u == 29 and SS.i == 4
                for i in range(22):
                    for z in range(2):
                        ch = i + 22 * z
                        sw_ = SS.next()
                        pz = bank((2 * i + z) % 8)
                        for kc in range(8):
                            B.mm(pz, sw_[:, kc, :], hT[:, kc, :], start=(kc == 0), stop=(kc == 7))
                        pb = pbuf[2 * (i % 2) + z]
                        fa = facc[2 * (i % 2) + z]
                        B.copy("act", pb[:, 0:2], up_hist[:, ch, :])
                        B.copy("act", pb[:, 2:2 + T], pz)
                        B.act(fa, pz, AF.Identity, bias=vl(V_FB + ch), scale=vl(V_FW + ch * 3 + 2))
                        B.copy("act", up_hist[:, ch, :], pb[:, T:T + 2])
                        B.stt(fa, pb[:, 0:T], vl(V_FW + ch * 3), fa, ALU.mult, ALU.add)
                        B.stt(fa, pb[:, 1:1 + T], vl(V_FW + ch * 3 + 1), fa, ALU.mult, ALU.add)
                    B.act(fsg[i % 2], facc[2 * (i % 2)], AF.Silu)
                    B.tt("pool", ffT[:, i, :], fsg[i % 2], facc[2 * (i % 2) + 1], ALU.mult)
                xo = [AR.alloc([128, T], F32) for _ in range(3)]
                if gi + 1 < n_groups:
                    nxt = (src[gi + 1], (gi + 1) * T)
                elif l + 1 < L:
                    nxt = (res_v[l % 2][0], 0)
                else:
                    nxt = None
                for n in range(8):
                    pz = bank(4 + n % 2)
                    first = True
                    for kg in range(3):
                        sd = SS.next()
                        for kc in range(8):
                            k = kg * 8 + kc
                            if k >= 22:
                                break
                            B.mm(pz, sd[:, kc, :], ffT[:, k, :], start=first, stop=(k == 21))
                            first = False
                    B.tt("dve", xo[n % 3], pz, xg[:, n, :], ALU.add)
                    B.dma(V(dst[gi].ap[n, :, t0:t0 + T], dst[gi].bufs), xo[n % 3], eng="act")
                    if nxt is not None:
                        B.dma(xg[:, n, :], V(nxt[0].ap[n, :, nxt[1]:nxt[1] + T], nxt[0].bufs), eng="act")
                prefetched["x"] = nxt is not None
                assert SS.u == NUNITS and SS.i == 4

        toks = [(("d", k), S.dma_val[k], "dma") for k in range(NDMA) if S.dma_val[k] > 0]
        S.final_wait("sp", toks)

        with nc.Block() as block:
            S.emit(nc, block, sems)
    return nc


def _rhs_unit(W, cols):
    u = np.zeros((128, 8, 512), np.float32)
    sub = W[:, cols]
    u[:, :, :len(cols)] = sub.reshape(8, 128, len(cols)).transpose(1, 0, 2)
    return u.reshape(128, 4096)


def _slab(Wk):
    nk = Wk.shape[0] // 128
    s = np.zeros((128, 8, 128), np.float32)
    s[:, :nk, :] = Wk.reshape(nk, 128, 128).transpose(1, 0, 2)
    return s


def _pack_layer(inp, l):
    w_in = inp["w_in"][l]
    units = []
    units.append(_rhs_unit(w_in, list(range(0, 512))))
    units.append(_rhs_unit(w_in, list(range(512, 1024))))
    units.append(_rhs_unit(w_in, list(range(1024, 1536))))
    units.append(_rhs_unit(w_in, list(range(1536, 2048))))
    units.append(_rhs_unit(w_in, list(range(2048, 2120))))
    slabs = []
    A0, C0, G0 = 2120, 4168, 7240
    for i in range(8):
        slabs.append(_slab(w_in[:, A0 + i * 128:A0 + (i + 1) * 128]))
        slabs.append(_slab(w_in[:, A0 + 1024 + i * 128:A0 + 1024 + (i + 1) * 128]))
    for i in range(8):
        for z in range(3):
            c0 = C0 + z * 1024 + i * 128
            slabs.append(_slab(w_in[:, c0:c0 + 128]))
    for n in range(8):
        for z in range(3):
            c0 = G0 + z * 1024 + n * 128
            slabs.append(_slab(w_in[:, c0:c0 + 128]))
        slabs.append(_slab(inp["w_conf_out"][l][:, n * 128:(n + 1) * 128]))
        slabs.append(_slab(inp["w_attn_out"][l][:, n * 128:(n + 1) * 128]))
        slabs.append(_slab(inp["w_sc_out"][l][:, n * 128:(n + 1) * 128]))
    for n in range(8):
        slabs.append(_slab(inp["w_o"][l][:, n * 128:(n + 1) * 128]))
    w_up = inp["w_up"][l]
    for i in range(22):
        slabs.append(_slab(w_up[:, i * 128:(i + 1) * 128]))
        slabs.append(_slab(w_up[:, FFN + i * 128:FFN + (i + 1) * 128]))
    w_dn = inp["w_down"][l]
    for n in range(8):
        for kg in range(3):
            k0, k1 = kg * 1024, min((kg + 1) * 1024, FFN)
            slabs.append(_slab(w_dn[k0:k1, n * 128:(n + 1) * 128]))
    assert len(slabs) == 41 * 4, len(slabs)
    for j in range(0, len(slabs), 4):
        units.append(np.stack(slabs[j:j + 4], axis=1).reshape(128, 4096))
    out = np.stack(units, axis=0)
    assert out.shape == (NUNITS, 128, 4096)
    return out


def _pack_vecs(inp, l):
    v = np.zeros((128, NV), np.float32)
    v[:, V_N1:V_N1 + 8] = inp["norm1_g"][l].reshape(8, 128).T
    v[:, V_N2:V_N2 + 8] = inp["norm2_g"][l].reshape(8, 128).T
    cw = inp["conf_conv_w"][l]
    v[:, V_CW:V_CW + 248] = cw.reshape(31, 8, 128).transpose(2, 1, 0).reshape(128, 248)
    v[:, V_CB:V_CB + 8] = inp["conf_conv_b"][l].reshape(8, 128).T
    v[:, V_LG:V_LG + 8] = inp["conf_ln_g"][l].reshape(8, 128).T
    v[:, V_LB:V_LB + 8] = inp["conf_ln_b"][l].reshape(8, 128).T
    v[:, V_SW:V_SW + 24] = inp["sc_conv_w"][l].reshape(3, 8, 128).transpose(2, 1, 0).reshape(128, 24)
    v[:, V_FW:V_FW + 132] = inp["ffn_conv_w"][l].reshape(3, 44, 128).transpose(2, 1, 0).reshape(128, 132)
    v[:, V_FB:V_FB + 44] = inp["ffn_conv_b"][l].reshape(44, 128).T
    return v


def _consts():
    c = np.zeros((128, 384 + NIT + 2), np.float32)
    c[:, 0:128] = np.eye(128, dtype=np.float32)
    tl = np.arange(128)
    valid = (tl[None, :] <= tl[:, None])
    c[:, 128:256] = valid.astype(np.float32)
    c[:, 256:384] = np.where(valid, 0.0, NEG).astype(np.float32)
    for i in range(NIT + 2):
        c[:, 384 + i] = 2.002 * (2.0 ** -(i + 1))
    return c


def _rope_tables():
    def tab(rot):
        pos = np.arange(SEQ, dtype=np.float32)
        inv = np.power(np.float32(500000.0), -np.arange(0, rot, 2, dtype=np.float32) / np.float32(rot)).astype(np.float32)
        ang = (pos[:, None] * inv[None, :]).astype(np.float32)
        return np.cos(ang).astype(np.float32), np.sin(ang).astype(np.float32)
    ca, sa = tab(32)
    ci, si = tab(16)
    r = np.concatenate([ca, sa, ca, ci, si, ci], axis=1)
    return np.ascontiguousarray(r.reshape(32, 128, 72).transpose(1, 0, 2))


_CACHE = {}


def _get_nc(n_layers, n_groups):
    key = (n_layers, n_groups)
    if key not in _CACHE:
        _CACHE[key] = build_program(n_layers, n_groups)
    return _CACHE[key]


def _shared_inputs(inp, layers):
    wst = np.stack([_pack_layer(inp, l) for l in layers], axis=0)
    vecs = np.ascontiguousarray(np.stack([_pack_vecs(inp, l) for l in layers], axis=1))
    gqk = np.zeros((128, len(layers), 256), np.float32)
    for i, l in enumerate(layers):
        gqk[:, i, 0:128] = inp["q_norm_g"][l][None, :]
        gqk[:, i, 128:256] = inp["k_norm_g"][l][None, :]
    return {"wst": wst, "vecs": vecs, "gqk": gqk, "rope": _rope_tables(), "cst": _consts()}


def run_layers(xT_list, inp, layers, n_groups=SEQ // T):
    nc = _get_nc(len(layers), n_groups)
    shared = _shared_inputs(inp, layers)
    in_maps = []
    for xT in xT_list:
        m = dict(shared)
        m["xT"] = xT
        in_maps.append(m)
    res = run_bass_kernel_spmd(nc, in_maps, core_ids=list(range(len(xT_list))))
    return [r["yT"] for r in res.results]


FUSED = True


def kernel(**inputs):
    inp = {k: np.asarray(v, dtype=np.float32) for k, v in inputs.items()}
    x = inp["x"]
    nb = x.shape[0]
    xT = [np.ascontiguousarray(x[b].T).reshape(8, 128, SEQ) for b in range(nb)]
    if FUSED:
        xT = run_layers(xT, inp, list(range(DEPTH)))
    else:
        for l in range(DEPTH):
            xT = run_layers(xT, inp, [l])
    out = np.stack([np.ascontiguousarray(t.reshape(D, SEQ).T) for t in xT], axis=0)
    return out.astype(np.float32)
```
